# Optimizing a Trainium2 kernel written in Bass

```python
import math
import jax, jax.numpy as jnp
from jax import lax
import numpy as np

D_MODEL = 1024
BATCH = 8
SEQ = 2048
DEPTH = 2
DEC_BATCH = 4
DEC_SEQ = 4096
PAST_LEN = 128

N_HEADS_A = 8
N_KV_HEADS_A = 2
GROUP_A = N_HEADS_A // N_KV_HEADS_A
HEAD_DIM_A = 64
WINDOW = 128
BLOCK = 128
N_HEADS_B = 8
QK_NOPE_DIM = 64
QK_ROPE_DIM = 32
V_DIM_B = 64
Q_LORA_RANK = 384
KV_LORA_RANK = 256
ROPE_THETA = 10000.0
N_BUCKETS = 32
MAX_DISTANCE = 128
D_FF = 2816
FFN_RES_WEIGHT = 0.5
EPS = 1e-6

A_Q_COLS = N_HEADS_A * HEAD_DIM_A
A_KV_COLS = N_KV_HEADS_A * HEAD_DIM_A
SPLIT_SIZES = (A_Q_COLS, A_KV_COLS, A_KV_COLS, Q_LORA_RANK, KV_LORA_RANK, QK_ROPE_DIM, D_MODEL, D_MODEL)
IN_COLS = A_Q_COLS + 2 * A_KV_COLS + Q_LORA_RANK + KV_LORA_RANK + QK_ROPE_DIM + 2 * D_MODEL

kernel_name = 'hybrid_gated_swa_mla_encoder'


def rmsnorm(x, g):
    x32 = x.astype(jnp.float32)
    y = x32 * lax.rsqrt(jnp.mean(x32 * x32, axis=-1, keepdims=True) + EPS) * g.astype(jnp.float32)
    return y.astype(x.dtype)


def t5_bucket(rel):
    half = N_BUCKETS // 2
    max_exact = half // 2
    ret = jnp.where(rel > 0, half, 0)
    n = jnp.abs(rel)
    nf = jnp.maximum(n, 1).astype(jnp.float32)
    large = max_exact + (jnp.log(nf / max_exact) / math.log(MAX_DISTANCE / max_exact) * (half - max_exact)).astype(jnp.int32)
    large = jnp.minimum(large, half - 1)
    return ret + jnp.where(n < max_exact, n, large)


def rope_tables(S):
    inv_freq = ROPE_THETA ** (-jnp.arange(0, QK_ROPE_DIM, 2, dtype=jnp.float32) / QK_ROPE_DIM)
    ang = jnp.arange(S, dtype=jnp.float32)[:, None] * inv_freq[None, :]
    return jnp.cos(ang), jnp.sin(ang)


def apply_rope(x, cos, sin):
    half = x.shape[-1] // 2
    x1, x2 = x[..., :half], x[..., half:]
    cos = cos.astype(x.dtype)
    sin = sin.astype(x.dtype)
    return jnp.concatenate([x1 * cos - x2 * sin, x1 * sin + x2 * cos], axis=-1)


def window_gqa(q, k, v, rel_bias, sink):
    B, S, _ = q.shape
    nb = S // BLOCK
    qb = q.reshape(B, nb, BLOCK, N_KV_HEADS_A, GROUP_A, HEAD_DIM_A)
    pad = ((0, 0), (BLOCK, BLOCK), (0, 0))
    kp = jnp.pad(k, pad).reshape(B, nb + 2, BLOCK, N_KV_HEADS_A, HEAD_DIM_A)
    vp = jnp.pad(v, pad).reshape(B, nb + 2, BLOCK, N_KV_HEADS_A, HEAD_DIM_A)
    kw = jnp.concatenate([kp[:, :-2], kp[:, 1:-1], kp[:, 2:]], axis=2)
    vw = jnp.concatenate([vp[:, :-2], vp[:, 1:-1], vp[:, 2:]], axis=2)
    s = jnp.einsum('bnqkgd,bnskd->bnkgqs', qb, kw).astype(jnp.float32) * (HEAD_DIM_A ** -0.5)
    qi = jnp.arange(BLOCK, dtype=jnp.int32)[:, None]
    si = jnp.arange(3 * BLOCK, dtype=jnp.int32)[None, :]
    rel = si - BLOCK - qi
    bias = rel_bias[t5_bucket(rel)].astype(jnp.float32)
    bias = jnp.transpose(bias, (2, 0, 1)).reshape(N_KV_HEADS_A, GROUP_A, BLOCK, 3 * BLOCK)
    s = s + bias[None, None]
    kpos = (jnp.arange(nb, dtype=jnp.int32)[:, None] - 1) * BLOCK + jnp.arange(3 * BLOCK, dtype=jnp.int32)[None, :]
    valid = (kpos >= 0) & (kpos < S)
    mask = valid[:, None, :] & (jnp.abs(rel) <= WINDOW)[None]
    s = jnp.where(mask[None, :, None, None], s, -jnp.inf)
    sk = sink.astype(jnp.float32).reshape(N_KV_HEADS_A, GROUP_A)[None, None, :, :, None]
    lse = jnp.logaddexp(jax.nn.logsumexp(s, axis=-1), sk)
    p = jnp.exp(s - lse[..., None]).astype(v.dtype)
    o = jnp.einsum('bnkgqs,bnskd->bnqkgd', p, vw)
    return o.reshape(B, S, A_Q_COLS)


def mla(cq, ckv, kr, q_norm_g, kv_norm_g, w_uq, w_ukv):
    B, S, _ = cq.shape
    q = (rmsnorm(cq, q_norm_g) @ w_uq).reshape(B, S, N_HEADS_B, QK_NOPE_DIM + QK_ROPE_DIM)
    kv = (rmsnorm(ckv, kv_norm_g) @ w_ukv).reshape(B, S, N_HEADS_B, QK_NOPE_DIM + V_DIM_B)
    q_nope, q_rope = q[..., :QK_NOPE_DIM], q[..., QK_NOPE_DIM:]
    k_nope, v = kv[..., :QK_NOPE_DIM], kv[..., QK_NOPE_DIM:]
    cos, sin = rope_tables(S)
    q_rope = apply_rope(q_rope, cos[:, None, :], sin[:, None, :])
    k_rope = apply_rope(kr, cos, sin)
    scale = (QK_NOPE_DIM + QK_ROPE_DIM) ** -0.5
    nb = S // BLOCK
    qn_b = q_nope.reshape(B, nb, BLOCK, N_HEADS_B, QK_NOPE_DIM).transpose(1, 0, 2, 3, 4)
    qr_b = q_rope.reshape(B, nb, BLOCK, N_HEADS_B, QK_ROPE_DIM).transpose(1, 0, 2, 3, 4)

    def attend(blk):
        qn, qr = blk
        s = jnp.einsum('bqhd,bshd->bhqs', qn, k_nope) + jnp.einsum('bqhr,bsr->bhqs', qr, k_rope)
        p = jax.nn.softmax(s.astype(jnp.float32) * scale, axis=-1).astype(v.dtype)
        return jnp.einsum('bhqs,bshd->bqhd', p, v)

    o = lax.map(attend, (qn_b, qr_b))
    return o.transpose(1, 0, 2, 3, 4).reshape(B, S, N_HEADS_B * V_DIM_B)


def half_step_ffn(x, pre_g, post_g, w_gate, w_up, w_down):
    h = rmsnorm(x, pre_g)
    y = (jax.nn.silu(h @ w_gate) * (h @ w_up)) @ w_down
    return x + FFN_RES_WEIGHT * rmsnorm(y, post_g)


def split_cols(proj):
    idx = []
    acc = 0
    for sz in SPLIT_SIZES[:-1]:
        acc += sz
        idx.append(acc)
    return jnp.split(proj, idx, axis=-1)


def token_mixer(x, rel_bias, pre_g, post_g, w_in, sink, q_norm_g, kv_norm_g, w_uq, w_ukv, w_a_out, w_b_out, w_o):
    h = rmsnorm(x, pre_g)
    qa, ka, va, cq, ckv, kr, ga, gb = split_cols(h @ w_in)
    o_a = window_gqa(qa, ka, va, rel_bias, sink) @ w_a_out
    o_b = mla(cq, ckv, kr, q_norm_g, kv_norm_g, w_uq, w_ukv) @ w_b_out
    merged = jax.nn.sigmoid(ga) * o_a + jax.nn.sigmoid(gb) * o_b
    return x + rmsnorm(merged @ w_o, post_g)


def trunk(x, rel_bias,
          ffn1_pre_g, ffn1_post_g, ffn1_w_gate, ffn1_w_up, ffn1_w_down,
          mix_pre_g, mix_post_g, w_in, sink, q_norm_g, kv_norm_g, w_uq, w_ukv, w_a_out, w_b_out, w_o,
          ffn2_pre_g, ffn2_post_g, ffn2_w_gate, ffn2_w_up, ffn2_w_down):
    for l in range(DEPTH):
        x = half_step_ffn(x, ffn1_pre_g[l], ffn1_post_g[l], ffn1_w_gate[l], ffn1_w_up[l], ffn1_w_down[l])
        x = token_mixer(x, rel_bias, mix_pre_g[l], mix_post_g[l], w_in[l], sink[l], q_norm_g[l], kv_norm_g[l],
                        w_uq[l], w_ukv[l], w_a_out[l], w_b_out[l], w_o[l])
        x = half_step_ffn(x, ffn2_pre_g[l], ffn2_post_g[l], ffn2_w_gate[l], ffn2_w_up[l], ffn2_w_down[l])
    return x


def setup_inputs(seed: int = 0) -> dict:
    key = jax.random.key(seed)
    ks = jax.random.split(key, 32)
    f32 = jnp.float32

    def w(k, shape, fan_in):
        return jax.random.normal(k, shape, f32) * (fan_in ** -0.5)

    def gain(k, shape):
        return 1.0 + 0.05 * jax.random.normal(k, shape, f32)

    D = D_MODEL
    return {
        'x_prompt': jax.random.normal(ks[0], (BATCH, SEQ, D), f32),
        'x_sample': jax.random.normal(ks[1], (DEC_BATCH, DEC_SEQ, D), f32),
        'rel_bias': 0.5 * jax.random.normal(ks[2], (N_BUCKETS, N_HEADS_A), f32),
        'ffn1_pre_g': gain(ks[3], (DEPTH, D)),
        'ffn1_post_g': gain(ks[4], (DEPTH, D)),
        'ffn1_w_gate': w(ks[5], (DEPTH, D, D_FF), D),
        'ffn1_w_up': w(ks[6], (DEPTH, D, D_FF), D),
        'ffn1_w_down': w(ks[7], (DEPTH, D_FF, D), D_FF),
        'mix_pre_g': gain(ks[8], (DEPTH, D)),
        'mix_post_g': gain(ks[9], (DEPTH, D)),
        'w_in': w(ks[10], (DEPTH, D, IN_COLS), D),
        'sink': jax.random.normal(ks[11], (DEPTH, N_HEADS_A), f32),
        'q_norm_g': gain(ks[12], (DEPTH, Q_LORA_RANK)),
        'kv_norm_g': gain(ks[13], (DEPTH, KV_LORA_RANK)),
        'w_uq': w(ks[14], (DEPTH, Q_LORA_RANK, N_HEADS_B * (QK_NOPE_DIM + QK_ROPE_DIM)), Q_LORA_RANK),
        'w_ukv': w(ks[15], (DEPTH, KV_LORA_RANK, N_HEADS_B * (QK_NOPE_DIM + V_DIM_B)), KV_LORA_RANK),
        'w_a_out': w(ks[16], (DEPTH, A_Q_COLS, D), A_Q_COLS),
        'w_b_out': w(ks[17], (DEPTH, N_HEADS_B * V_DIM_B, D), N_HEADS_B * V_DIM_B),
        'w_o': w(ks[18], (DEPTH, D, D), D),
        'ffn2_pre_g': gain(ks[19], (DEPTH, D)),
        'ffn2_post_g': gain(ks[20], (DEPTH, D)),
        'ffn2_w_gate': w(ks[21], (DEPTH, D, D_FF), D),
        'ffn2_w_up': w(ks[22], (DEPTH, D, D_FF), D),
        'ffn2_w_down': w(ks[23], (DEPTH, D_FF, D), D_FF),
    }


def reference(x_prompt, x_sample, rel_bias,
              ffn1_pre_g, ffn1_post_g, ffn1_w_gate, ffn1_w_up, ffn1_w_down,
              mix_pre_g, mix_post_g, w_in, sink, q_norm_g, kv_norm_g, w_uq, w_ukv, w_a_out, w_b_out, w_o,
              ffn2_pre_g, ffn2_post_g, ffn2_w_gate, ffn2_w_up, ffn2_w_down):
    params = (rel_bias,
              ffn1_pre_g, ffn1_post_g, ffn1_w_gate, ffn1_w_up, ffn1_w_down,
              mix_pre_g, mix_post_g, w_in, sink, q_norm_g, kv_norm_g, w_uq, w_ukv, w_a_out, w_b_out, w_o,
              ffn2_pre_g, ffn2_post_g, ffn2_w_gate, ffn2_w_up, ffn2_w_down)
    y_prompt = trunk(x_prompt, *params)
    y_sample = trunk(x_sample, *params)
    return (y_prompt, y_sample)
```

```python
import math
from contextlib import ExitStack

import numpy as np
import concourse.bass as bass
import concourse.mybir as mybir
from concourse.bass_utils import run_bass_kernel_spmd

F32 = mybir.dt.float32
BF16 = mybir.dt.bfloat16
AF = mybir.ActivationFunctionType
ALU = mybir.AluOpType

D = 1024
NT = 4096
NSUB = NT // 128
DFF = 2816
NFC = DFF // 128
DEPTH = 2
EPS = 1e-6
IN_COLS = 3488
N_CORES = 8
SEM_ROT = 16000


class Buf:
    __slots__ = ("name", "w", "r", "rd", "excl")

    def __init__(self, name, excl=False):
        self.name = name
        self.excl = excl
        self.w = None
        self.r = {}
        self.rd = []


class DmaSem:
    def __init__(self, sched, name):
        self.h = sched.new_sem(name)
        self.cnt = 0


class EngQ:
    def __init__(self, sched, eng, name, is_pe=False, dma_only=False):
        self.sched = sched
        self.eng = eng
        self.name = name
        self.is_pe = is_pe
        self.cnt = 0
        self.sem = None
        self.nsem = 0
        self.seen = {}
        self.dma_only = dma_only

    def wait(self, tok, raw=True, rar=False):
        if tok is None:
            return
        sem, val, src = tok
        if src is self and (self.is_pe or rar):
            return
        key = id(sem)
        if self.seen.get(key, 0) >= val:
            return
        self.eng.wait_ge(sem, val)
        self.seen[key] = val

    def mark(self, ins):
        if self.sem is None or self.cnt >= SEM_ROT:
            self.sem = self.sched.new_sem(f"q_{self.name}_{self.nsem}")
            self.nsem += 1
            self.cnt = 0
        ins.then_inc(self.sem, 1)
        self.cnt += 1
        return (self.sem, self.cnt, self)


class Sched:
    def __init__(self, nc, stack):
        self.nc = nc
        self.stack = stack
        self.nsems = 0
        self.pe = EngQ(self, nc.tensor, "pe", is_pe=True)
        self.act = EngQ(self, nc.scalar, "act")
        self.dve = EngQ(self, nc.vector, "dve")
        self.pool = EngQ(self, nc.gpsimd, "pool")
        self.sp = EngQ(self, nc.sync, "sp", dma_only=True)
        self.queues = [self.pe, self.act, self.dve, self.pool, self.sp]
        self.dma_sems = []
        self.free_dsems = []
        self.free_dsems_sw = []
        self.phase_dsems = []

    def new_sem(self, name):
        self.nsems += 1
        return self.stack.enter_context(self.nc.semaphore(f"{name}_{self.nsems}"))

    def dsem(self, name, sw=False):
        pool = self.free_dsems_sw if sw else self.free_dsems
        if pool:
            s = pool.pop()
        else:
            s = DmaSem(self, name)
            s.sw = sw
            self.dma_sems.append(s)
        self.phase_dsems.append(s)
        return s

    def release_phase(self):
        for d in self.phase_dsems:
            (self.free_dsems_sw if d.sw else self.free_dsems).append(d)
        self.phase_dsems = []

    def _deps(self, q, reads, writes):
        for b in reads:
            q.wait(b.w, raw=True)
            if b.excl:
                for t in b.r.values():
                    q.wait(t, rar=True)
        for b in writes:
            q.wait(b.w, raw=True)
            for t in b.r.values():
                q.wait(t, raw=False)
            for t in b.rd:
                q.wait(t, raw=False)

    def op(self, q, reads, writes, fn):
        self._deps(q, reads, writes)
        ins = fn()
        tok = q.mark(ins)
        for b in reads:
            b.r[q.name] = tok
        for b in writes:
            b.w = tok
            b.r = {}
            b.rd = []
        return tok

    def dma(self, q, reads, writes, dsem, out, in_, **kw):
        assert dsem.sw == (q is self.pool), "DMA semaphore kind does not match the issuing queue"
        self._deps(q, reads, writes)
        ins = q.eng.dma_start(out=out, in_=in_, **kw)
        dsem.cnt += 16
        ins.then_inc(dsem.h, 16)
        tok = (dsem.h, dsem.cnt, None)
        for b in reads:
            b.rd.append(tok)
        for b in writes:
            b.w = tok
            b.r = {}
            b.rd = []
        return tok

    def barrier(self):
        toks = []
        for q in self.queues:
            if q.sem is not None and q.cnt > 0:
                toks.append((q.sem, q.cnt, None))
        for s in self.dma_sems:
            if s.cnt > 0:
                toks.append((s.h, s.cnt, None))
        for q in self.queues:
            for t in toks:
                q.wait(t)


class RR:
    def __init__(self, name, n):
        self.bufs = [Buf(f"{name}{i}") for i in range(n)]
        self.n = n
        self.i = 0
        self.base = 0

    def next(self):
        k = self.i % self.n
        self.i += 1
        return k + self.base, self.bufs[k]


WNAMES = ["ffn1_pre_g", "ffn1_post_g", "ffn1_w_gate", "ffn1_w_up", "ffn1_w_down",
          "mix_pre_g", "mix_post_g", "w_in", "sink", "q_norm_g", "kv_norm_g", "w_uq", "w_ukv",
          "w_a_out", "w_b_out", "w_o",
          "ffn2_pre_g", "ffn2_post_g", "ffn2_w_gate", "ffn2_w_up", "ffn2_w_down"]
WSHAPES = {
    "ffn1_pre_g": [DEPTH, D], "ffn1_post_g": [DEPTH, D], "ffn1_w_gate": [DEPTH, D, DFF],
    "ffn1_w_up": [DEPTH, D, DFF], "ffn1_w_down": [DEPTH, DFF, D],
    "mix_pre_g": [DEPTH, D], "mix_post_g": [DEPTH, D], "w_in": [DEPTH, D, IN_COLS], "sink": [DEPTH, 8],
    "q_norm_g": [DEPTH, 384], "kv_norm_g": [DEPTH, 256], "w_uq": [DEPTH, 384, 768],
    "w_ukv": [DEPTH, 256, 1024], "w_a_out": [DEPTH, 512, D], "w_b_out": [DEPTH, 512, D],
    "w_o": [DEPTH, D, D],
    "ffn2_pre_g": [DEPTH, D], "ffn2_post_g": [DEPTH, D], "ffn2_w_gate": [DEPTH, D, DFF],
    "ffn2_w_up": [DEPTH, D, DFF], "ffn2_w_down": [DEPTH, DFF, D],
}


def build_program(n_phases=8, dbg_stop=None):
    nc = bass.Bass("TRN2", target_bir_lowering=False)
    x_in = nc.dram_tensor("x", [NT, D], F32, kind="ExternalInput").ap()
    y = nc.dram_tensor("y", [NT, D], F32, kind="ExternalOutput").ap()
    class LazyW(dict):
        def __missing__(self, n):
            self[n] = nc.dram_tensor(n, WSHAPES[n], F32, kind="ExternalInput").ap()
            return self[n]
    W = LazyW()
    WSHAPES["rel_bias"] = [32, 8]
    nc.used_weights = W
    CSH = {"c_ident": [128, 128], "c_onehot2": [33, 640], "c_rope": [NT, 32], "c_flags": [128, 2]}

    class LazyC(dict):
        def __missing__(self, n):
            self[n] = nc.dram_tensor(n, CSH[n], F32, kind="ExternalInput").ap()
            return self[n]
    C = LazyC()
    nc.used_consts = C

    with ExitStack() as stack:
        S = Sched(nc, stack)
        block = stack.enter_context(nc.Block())
        ps = stack.enter_context(nc.psum_tensor("ps", [128, 8, 512], F32))
        pb = [Buf(f"pb{i}", excl=True) for i in range(8)]
        xsrc0 = [Buf(f"xi{s}") for s in range(NSUB)]
        yb = [Buf(f"y{s}") for s in range(NSUB)]

        _uid = [0]

        def SB(stk, name, shape, dt):
            _uid[0] += 1
            return stk.enter_context(nc.sbuf_tensor(f"{name}_{_uid[0]}", shape, dt))

        ident = stack.enter_context(nc.sbuf_tensor("ident", [128, 128], BF16))
        identb = Buf("ident")
        S.dma(S.pool, [], [identb], S.dsem("ident", sw=True), ident[:], C["c_ident"][:, :])

        def rstd_ops(ss_ap, rs_ap, ssb, rsb, n):
            S.op(S.act, [ssb, epsb], [rsb],
                 lambda: nc.scalar.activation(out=rs_ap, in_=ss_ap, func=AF.Ln, scale=1.0 / n, bias=eps_t[:, 0:1]))
            S.op(S.act, [rsb], [rsb],
                 lambda: nc.scalar.activation(out=rs_ap, in_=rs_ap, func=AF.Exp, scale=-0.5))

        eps_t = stack.enter_context(nc.sbuf_tensor("eps_t", [128, 1], F32))
        epsb = Buf("eps")
        S.op(S.pool, [], [epsb], lambda: nc.gpsimd.memset(eps_t[:], EPS))

        def ffn_phase(l, pfx, src, srcb, dst, dstb):
            wg = W[f"{pfx}_w_gate"][l].rearrange("(k p) f -> p k f", p=128)
            wu = W[f"{pfx}_w_up"][l].rearrange("(k p) f -> p k f", p=128)
            wd = W[f"{pfx}_w_down"][l].rearrange("(f p) n -> p f n", p=128)
            with ExitStack() as st:
                T = 2048
                hT = SB(st, "hT", [128, 8, T], BF16)
                aT = SB(st, "aT", [128, NFC, T], BF16)
                wdt = SB(st, "wdt", [128, NFC, D], BF16)
                wgu = SB(st, "wgu", [128, 3, 2, 8, 128], BF16)
                xin = SB(st, "xin", [128, 3, D], F32)
                hn = SB(st, "hn", [128, 2, D], BF16)
                sg = SB(st, "sg", [128, 2, 512], BF16)
                junk = sg[:, :, :].rearrange("p a b -> p (a b)")
                ybuf = SB(st, "ybuf", [128, 2, D], F32)
                gcol = SB(st, "gcol", [128, 8], F32)
                gpost = SB(st, "gpost", [128, D], F32)
                stat = SB(st, "stat", [128, 16], F32)

                hTev = [Buf(f"hTe{i}") for i in range(4)]
                hTod = [Buf(f"hTo{i}") for i in range(4)]
                aTb = [Buf(f"aT{i}") for i in range(4)]
                wdb = [Buf("wd")]
                wgu_rr = RR("wgu", 3)
                wgu_sem = [S.dsem(f"wgu{i}", sw=True) for i in range(3)]
                xin_rr = RR("xin", 3)
                xin_rr3 = RR("xin3", 2)
                xin_rr3.bufs = xin_rr.bufs[0:2]
                xin_rr1 = RR("xin1", 1)
                xin_rr1.bufs = xin_rr.bufs[2:3]
                xin_rr1.base = 2
                xin_sem = [S.dsem(f"xin{i}") for i in range(3)]
                hn_rr = RR("hn", 2)
                sg_rr = RR("sg", 2)
                y_rr = RR("ybuf", 2)
                y_sem = [S.dsem(f"yst{i}") for i in range(2)]
                gsem = S.dsem("g")
                gsem2 = S.dsem("g2")
                wd_sem = S.dsem("wd", sw=True)
                gpreb, gpostb = Buf("gpre"), Buf("gpost")
                statb = [Buf(f"stat{i}") for i in range(16)]

                S.dma(S.sp, [], [gpreb], gsem, gcol[:, :], W[f"{pfx}_pre_g"][l].rearrange("(k p) -> p k", p=128),
                      allow_slow_non_contiguous=True)
                S.dma(S.sp, [], [gpostb], gsem2, gpost[:], W[f"{pfx}_post_g"][l:l + 1, :].to_broadcast([128, D]))
                S.op(S.pool, [gpostb], [gpostb],
                     lambda: nc.gpsimd.tensor_scalar(out=gpost[:], in0=gpost[:], scalar1=0.5, scalar2=None, op0=ALU.mult))
                wd_issued = [False]

                def issue_wd():
                    if wd_issued[0]:
                        return
                    wd_issued[0] = True
                    for fc in range(NFC):
                        S.dma(S.pool, [], [wdb[0]], wd_sem, wdt[:, fc, :], wd[:, fc, :])

                cvt["gen"] = None
                gu_plan = []
                for stn in range(NT // T):
                    for fc in range(NFC):
                        gu_plan.append((stn, fc))
                gu_state = {"next": 0, "slots": {}}

                def issue_gu(upto):
                    while gu_state["next"] <= min(upto, len(gu_plan) - 1):
                        i = gu_state["next"]
                        _, fc = gu_plan[i]
                        k, b = wgu_rr.next()
                        S.dma(S.pool, [], [b], wgu_sem[k], wgu[:, k, 0, :, :], wg[:, :, fc * 128:(fc + 1) * 128])
                        S.dma(S.pool, [], [b], wgu_sem[k], wgu[:, k, 1, :, :], wu[:, :, fc * 128:(fc + 1) * 128])
                        gu_state["slots"][i] = (k, b)
                        gu_state["next"] += 1

                if dbg_stop == "gains":
                    S.barrier()
                    return
                issue_gu(1)
                gcnt = 0
                ycnt = 0
                s1st = {}

                def s1_front(stn, s, rr):
                    sub = stn * (T // 128) + s
                    k, xb = rr.next()
                    S.dma(S.sp, [srcb[sub]], [xb], xin_sem[k], xin[:, k, :], src[sub * 128:(sub + 1) * 128, :])
                    sc = k
                    S.op(S.act, [xb], [statb[sc]] + sg_rr.bufs,
                         lambda: nc.scalar.activation(out=junk[:], in_=xin[:, k, :], func=AF.Square,
                                                      accum_out=stat[:, sc:sc + 1]))
                    rstd_ops(stat[:, sc:sc + 1], stat[:, sc + 3:sc + 4], statb[sc], statb[sc + 3], D)
                    hk, hb = hn_rr.next()
                    S.op(S.dve, [xb, statb[sc + 3]], [hb],
                         lambda: nc.vector.tensor_scalar(out=hn[:, hk, :], in0=xin[:, k, :],
                                                         scalar1=stat[:, sc + 3:sc + 4], scalar2=None,
                                                         op0=ALU.mult))
                    s1st[(stn, s)] = (hk, hb)

                def s1_back(stn, s):
                    hk, hb = s1st.pop((stn, s))
                    tb = 6 + (s % 2)
                    pst = ps[:, tb, :].bitcast(BF16)

                    def tr():
                        for kk in range(8):
                            ins = nc.tensor.transpose(out=pst[:, kk * 128:(kk + 1) * 128],
                                                      in_=hn[:, hk, kk * 128:(kk + 1) * 128], identity=ident[:])
                        return ins
                    S.op(S.pe, [hb, identb], [pb[tb]], tr)

                    def cpa():
                        for kk in range(0, 8, 1):
                            ins = nc.scalar.activation(out=hT[:, kk, s * 128:(s + 1) * 128],
                                                       in_=pst[:, kk * 128:(kk + 1) * 128], func=AF.Copy,
                                                       scale=gcol[:, kk:kk + 1])
                        return ins

                    def cpv():
                        for kk in range(0, 8, 1):
                            ins = nc.vector.tensor_scalar(out=hT[:, kk, s * 128:(s + 1) * 128],
                                                          in0=pst[:, kk * 128:(kk + 1) * 128],
                                                          scalar1=gcol[:, kk:kk + 1], scalar2=None, op0=ALU.mult)
                        return ins
                    hTe = hTev[s // 4]
                    hTo = hTod[s // 4]
                    if tb == 6:
                        S.op(S.act, [pb[tb], gpreb], [hTe, hTo], cpa)
                    else:
                        S.op(S.dve, [pb[tb], gpreb], [hTe, hTo], cpv)

                NST = NT // T
                for stn in range(NST):
                    if stn == 0:
                        for s in range(T // 128):
                            s1_front(stn, s, xin_rr)
                            s1_back(stn, s)
                    for fc in range(NFC):
                        gi = stn * NFC + fc
                        issue_gu(gi + 2)
                        if gi == 1:
                            issue_wd()
                        if l == 0 and n_phases >= 3 and gi >= 3:
                            if cvt["gen"] is None:
                                cvt["gen"] = convert_mixer_weights(0 if pfx == "ffn1" else 1)
                            next(cvt["gen"], None)
                        wk, wb = gu_state["slots"][gi]
                        for tt in range(T // 512):
                            gbk = 2 * (gcnt % 2)
                            gcnt += 1

                            def mm(which, bank):
                                for kk in range(8):
                                    ins = nc.tensor.matmul(ps[:, bank, :], lhsT=wgu[:, wk, which, kk, :],
                                                           rhs=hT[:, kk, tt * 512:(tt + 1) * 512],
                                                           start=(kk == 0), stop=(kk == 7))
                                return ins
                            S.op(S.pe, [wb, hTev[tt], hTod[tt]], [pb[gbk]], lambda: mm(0, gbk))
                            S.op(S.pe, [wb, hTev[tt], hTod[tt]], [pb[gbk + 1]], lambda: mm(1, gbk + 1))
                            sk, sb = sg_rr.next()
                            S.op(S.act, [pb[gbk]], [sb],
                                 lambda: nc.scalar.activation(out=sg[:, sk, :], in_=ps[:, gbk, :], func=AF.Silu))
                            S.op(S.dve, [sb, pb[gbk + 1]], [aTb[tt]],
                                 lambda: nc.vector.tensor_tensor(out=aT[:, fc, tt * 512:(tt + 1) * 512],
                                                                 in0=sg[:, sk, :], in1=ps[:, gbk + 1, :], op=ALU.mult))
                    issue_wd()
                    if stn == NST - 1 and cvt["gen"] is not None:
                        for _ in cvt["gen"]:
                            pass
                    if dbg_stop == "s2":
                        S.barrier()
                        return
                    nsub = T // 128
                    pend = {}

                    inter = (stn + 1 < NST)
                    rr3 = xin_rr3 if inter else xin_rr
                    la3 = 1 if inter else 2

                    def ld(s):
                        if s >= nsub:
                            return
                        sub = stn * nsub + s
                        k, xb = rr3.next()
                        S.dma(S.sp, [srcb[sub]], [xb], xin_sem[k], xin[:, k, :], src[sub * 128:(sub + 1) * 128, :])
                        pend[s] = (k, xb)
                    for s_ in range(la3):
                        ld(s_)
                    if inter:
                        s1_front(stn + 1, 0, xin_rr1)
                    for s in range(nsub):
                        sub = stn * nsub + s
                        ld(s + la3)
                        xk, xb = pend.pop(s)
                        yk, ybb = y_rr.next()
                        for half in range(2):
                            bank = 4 + (ycnt % 2)
                            ycnt += 1

                            def dn():
                                for fc in range(NFC):
                                    ins = nc.tensor.matmul(ps[:, bank, :], lhsT=aT[:, fc, s * 128:(s + 1) * 128],
                                                           rhs=wdt[:, fc, half * 512:(half + 1) * 512],
                                                           start=(fc == 0), stop=(fc == NFC - 1))
                                return ins
                            S.op(S.pe, [aTb[s // 4]] + wdb, [pb[bank]], dn)
                            S.op(S.act, [pb[bank]], [ybb],
                                 lambda: nc.scalar.copy(out=ybuf[:, yk, half * 512:(half + 1) * 512], in_=ps[:, bank, :]))
                            c = 8 + yk * 2 + half
                            S.op(S.act, [pb[bank]], [statb[c], sg_rr.bufs[0]],
                                 lambda: nc.scalar.activation(out=junk[:, 0:512], in_=ps[:, bank, :], func=AF.Square,
                                                              accum_out=stat[:, c:c + 1]))
                        if inter:
                            s1_back(stn + 1, s)
                            if s + 1 < nsub:
                                s1_front(stn + 1, s + 1, xin_rr1)
                        c0 = 8 + yk * 2
                        c2 = 12 + yk
                        S.op(S.dve, [statb[c0], statb[c0 + 1]], [statb[c2]],
                             lambda: nc.vector.tensor_tensor(out=stat[:, c2:c2 + 1], in0=stat[:, c0:c0 + 1],
                                                             in1=stat[:, c0 + 1:c0 + 2], op=ALU.add))
                        rstd_ops(stat[:, c2:c2 + 1], stat[:, c2 + 2:c2 + 3], statb[c2], statb[c2 + 2], D)
                        S.op(S.dve, [ybb, statb[c2 + 2], gpostb], [ybb],
                             lambda: nc.vector.scalar_tensor_tensor(out=ybuf[:, yk, :], in0=ybuf[:, yk, :],
                                                                    scalar=stat[:, c2 + 2:c2 + 3], in1=gpost[:],
                                                                    op0=ALU.mult, op1=ALU.mult))
                        S.op(S.dve, [ybb, xb], [ybb],
                             lambda: nc.vector.tensor_tensor(out=ybuf[:, yk, :], in0=ybuf[:, yk, :],
                                                             in1=xin[:, xk, :], op=ALU.add))
                        S.dma(S.sp, [ybb], [dstb[sub]], y_sem[yk], dst[sub * 128:(sub + 1) * 128, :], ybuf[:, yk, :])
                S.barrier()

        kbT_d = nc.dram_tensor("kbT_d", [8, 96, NT], BF16, kind="Internal").ap()
        vb_d = nc.dram_tensor("vb_d", [128, 8, NSUB, 65], BF16, kind="Internal").ap()
        kaT_d = nc.dram_tensor("kaT_d", [64, 2, NT], BF16, kind="Internal").ap()
        va_d = nc.dram_tensor("va_d", [128, 2, NSUB, 65], BF16, kind="Internal").ap()
        ev_d = nc.dram_tensor("ev_d", [8, 640], BF16, kind="Internal").ap()
        wab_d = nc.dram_tensor("wab_d", [DEPTH, 8, 64, 2, 8, 128], BF16, kind="Internal").ap()
        wg_d = nc.dram_tensor("wg_d", [DEPTH, 8, 128, 2, 8, 128], BF16, kind="Internal").ap()
        wcv_b = [Buf("wcv0"), Buf("wcv1")]

        def convert_mixer_weights(l):
            sem = S.dsem(f"wcv{l}", sw=True)
            win_l = W["w_in"][l].rearrange("(k p) c -> p k c", p=128)
            wav = W["w_a_out"][l].rearrange("(h d) n -> d h n", d=64)
            wbv = W["w_b_out"][l].rearrange("(h d) n -> d h n", d=64)
            for dc in range(8):
                cs_ = slice(dc * 128, (dc + 1) * 128)
                S.dma(S.pool, [], [wcv_b[l]], sem, wab_d[l, dc, :, 0, :, :], wav[:, :, cs_])
                yield
                S.dma(S.pool, [], [wcv_b[l]], sem, wab_d[l, dc, :, 1, :, :], wbv[:, :, cs_])
                yield
                S.dma(S.pool, [], [wcv_b[l]], sem, wg_d[l, dc, :, 0, :, :], win_l[:, :, 1440 + dc * 128:1440 + (dc + 1) * 128])
                yield
                S.dma(S.pool, [], [wcv_b[l]], sem, wg_d[l, dc, :, 1, :, :], win_l[:, :, 2464 + dc * 128:2464 + (dc + 1) * 128])
                yield
        cvt = {"gen": None}

        kvd_b = [Buf(f"kvd{t}") for t in range(NT // 512)]
        biasd_b = Buf("biasd")
        ebd_b = Buf("ebd")

        def prologue():
            with ExitStack() as st:
                rb = SB(st, "rb", [33, 8], F32)
                oh = SB(st, "oh", [33, 640], F32)
                evs = SB(st, "evs", [8, 640], BF16)
                rbb, ohb, evb = Buf("rb"), Buf("oh"), Buf("evs")
                S.op(S.pool, [], [rbb], lambda: nc.gpsimd.memset(rb[:], -30000.0))
                S.dma(S.sp, [], [rbb], S.dsem("rb"), rb[0:32, :], W["rel_bias"][:, :])
                S.dma(S.sp, [], [ohb], S.dsem("oh"), oh[:, :], C["c_onehot2"][:, :])
                S.op(S.pe, [rbb, ohb], [pb[0]],
                     lambda: nc.tensor.matmul(ps[0:8, 0, :], lhsT=rb[:, :], rhs=oh[:, 0:512], start=True, stop=True))
                S.op(S.pe, [rbb, ohb], [pb[1]],
                     lambda: nc.tensor.matmul(ps[0:8, 1, 0:128], lhsT=rb[:, :], rhs=oh[:, 512:640], start=True, stop=True))
                S.op(S.act, [pb[0]], [evb],
                     lambda: nc.scalar.activation(out=evs[:, 0:512], in_=ps[0:8, 0, :], func=AF.Exp))
                S.op(S.act, [pb[1]], [evb],
                     lambda: nc.scalar.activation(out=evs[:, 512:640], in_=ps[0:8, 1, 0:128], func=AF.Exp))
                S.dma(S.sp, [evb], [ebd_b], S.dsem("evst"), ev_d[:, :], evs[:, :])
                S.barrier()

        def nt_front(R, sub, src, srcb):
            k, xb = R["xin_rr"].next()
            xin = R["xin"]
            stat = R["stat"]
            statb = R["statb"]
            S.dma(S.sp, [srcb[sub]], [xb], R["xin_sem"][k], xin[:, k, :], src[sub * 128:(sub + 1) * 128, :])
            sc = k
            S.op(S.act, [xb], [statb[sc]] + R["junkb"],
                 lambda: nc.scalar.activation(out=R["junk"], in_=xin[:, k, :], func=AF.Square,
                                              accum_out=stat[:, sc:sc + 1]))
            rstd_ops(stat[:, sc:sc + 1], stat[:, sc + 3:sc + 4], statb[sc], statb[sc + 3], D)
            hk, hb = R["hn_rr"].next()
            hn = R["hn"]
            S.op(S.dve, [xb, statb[sc + 3]], [hb],
                 lambda: nc.vector.tensor_scalar(out=hn[:, hk, :], in0=xin[:, k, :],
                                                 scalar1=stat[:, sc + 3:sc + 4], scalar2=None, op0=ALU.mult))
            return (hk, hb, sub)

        def nt_back(R, st_, hT, hTb, col0, eng=None):
            hk, hb, sub = st_
            hn = R["hn"]
            tb = 6 + (sub % 2)
            pst = ps[:, tb, :].bitcast(BF16)

            def tr():
                for kk in range(8):
                    ins = nc.tensor.transpose(out=pst[:, kk * 128:(kk + 1) * 128],
                                              in_=hn[:, hk, kk * 128:(kk + 1) * 128], identity=ident[:])
                return ins
            S.op(S.pe, [hb, identb], [pb[tb]], tr)

            def cpa():
                for kk in range(8):
                    ins = nc.scalar.activation(out=hT[:, kk, col0:col0 + 128],
                                               in_=pst[:, kk * 128:(kk + 1) * 128], func=AF.Copy,
                                               scale=R["gcol"][:, kk:kk + 1])
                return ins

            def cpv():
                for kk in range(8):
                    ins = nc.vector.tensor_scalar(out=hT[:, kk, col0:col0 + 128], in0=pst[:, kk * 128:(kk + 1) * 128],
                                                  scalar1=R["gcol"][:, kk:kk + 1], scalar2=None, op0=ALU.mult)
                return ins
            if tb == 6 and eng != "dve":
                S.op(S.act, [pb[tb], R["gcolb"]], [hTb], cpa)
            else:
                S.op(S.dve, [pb[tb], R["gcolb"]], [hTb], cpv)

        def norm_transpose(R, sub, src, srcb, hT, hTb, col0, keep_x=False):
            nt_back(R, nt_front(R, sub, src, srcb), hT, hTb, col0)

        def kv_phase(l):
            win = W["w_in"][l].rearrange("(k p) c -> p k c", p=128)
            with ExitStack() as st:
                hT2 = SB(st, "hT", [128, 2, 8, 512], BF16)
                wkv = SB(st, "wkv", [128, 8, 544], BF16)
                wukv = SB(st, "wukv", [128, 2, 1024], BF16)
                xin = SB(st, "xin", [128, 3, D], F32)
                hn = SB(st, "hn", [128, 2, D], BF16)
                junk = SB(st, "junk", [128, D], BF16)
                stat = SB(st, "stat", [128, 16], F32)
                gcol = SB(st, "gcol", [128, 8], F32)
                kvg = SB(st, "kvg", [128, 256], F32)
                cs = SB(st, "cs", [128, NSUB, 32], F32)
                ckvn = SB(st, "ckvn", [128, 4, 256], BF16)
                ckvnT = SB(st, "ckvnT", [128, 2, 512], BF16)
                ckvf = SB(st, "ckvf", [128, 4, 256], F32)
                ckvf_rr = RR("ckvf", 4)
                krt = SB(st, "krt", [128, 4, 96], BF16)
                rtmp = SB(st, "rtmp", [128, 4, 16], F32)
                kaT_sb = SB(st, "kaT_sb", [64, 2, 2, 512], BF16)
                va_sb = SB(st, "va_sb", [128, 2, 2, 4, 65], BF16)
                kbT_sb = SB(st, "kbT_sb", [96, 2, 8, 512], BF16)
                vb_sb = SB(st, "vb_sb", [128, 2, 8, 4, 65], BF16)

                R = dict(xin=xin, xin_rr=RR("xin", 3), xin_sem=[S.dsem(f"xin{i}") for i in range(3)], stat=stat,
                         statb=[Buf(f"stat{i}") for i in range(16)], hn=hn, hn_rr=RR("hn", 2), gcol=gcol,
                         gcolb=Buf("gcol"), junk=junk[:, :], junkb=[Buf("junk")])
                statb = R["statb"]
                S.dma(S.sp, [], [R["gcolb"]], S.dsem("gcol"), gcol[:, :],
                      W["mix_pre_g"][l].rearrange("(k p) -> p k", p=128), allow_slow_non_contiguous=True)
                kvgb, csb, wkvb, wukvb = Buf("kvg"), Buf("cs"), Buf("wkv"), Buf("wukv")
                S.dma(S.sp, [], [kvgb], S.dsem("kvg"), kvg[:, :], W["kv_norm_g"][l:l + 1, :].to_broadcast([128, 256]))
                S.dma(S.sp, [], [csb], S.dsem("cs"), cs[:, :, :], C["c_rope"].rearrange("(s p) e -> p s e", p=128))
                wsem = S.dsem("wkv", sw=True)
                for (c0, c1, o0) in ((512, 640, 0), (640, 768, 128), (1152, 1408, 256), (1408, 1440, 512)):
                    S.dma(S.pool, [], [wkvb], wsem, wkv[:, :, o0:o0 + (c1 - c0)], win[:, :, c0:c1])
                wu = W["w_ukv"][l].rearrange("(k p) (h t d) -> p k t h d", p=128, t=2, d=64)
                wsem2 = S.dsem("wukv", sw=True)
                for t in range(2):
                    for k in range(2):
                        S.dma(S.pool, [], [wukvb], wsem2,
                              wukv[:, k, t * 512:(t + 1) * 512].rearrange("p (h d) -> p h d", d=64), wu[:, k, t, :, :])
                onesb = Buf("ones")

                def ms():
                    nc.gpsimd.memset(krt[:], 0.0)
                    nc.gpsimd.memset(va_sb[:], 1.0)
                    return nc.gpsimd.memset(vb_sb[:], 1.0)
                S.op(S.pool, [], [onesb], ms)
                hTbs = [Buf("hT0"), Buf("hT1")]
                ckvnb = RR("ckvn", 4)
                ckvnTb = Buf("ckvnT")
                krtb = RR("krt", 4)
                rtb = Buf("rtmp")
                out_rr = RR("kvout", 2)
                st_sems = [[S.dsem(f"kvst{i}_{j}") for j in range(4)] for i in range(2)]
                bcnt = [0]

                def bank4():
                    b = bcnt[0] % 4
                    bcnt[0] += 1
                    return b

                for j in range(4):
                    norm_transpose(R, j, y, yb, hT2[:, 0, :, :], hTbs[0], j * 128)
                frs = {}
                for tq in range(NT // 512):
                    ok, ob = out_rr.next()
                    hT = hT2[:, tq % 2, :, :]
                    hTb = hTbs[tq % 2]
                    for kv in range(2):
                        bank = bank4()

                        def mm_a():
                            for kk in range(8):
                                ins = nc.tensor.matmul(ps[0:64, bank, :], lhsT=wkv[:, kk, kv * 64:(kv + 1) * 64],
                                                       rhs=hT[:, kk, :], start=(kk == 0), stop=(kk == 7))
                            return ins
                        S.op(S.pe, [wkvb, hTb], [pb[bank]], mm_a)
                        S.op(S.act, [pb[bank]], [ob],
                             lambda: nc.scalar.copy(out=kaT_sb[:, ok, kv, :], in_=ps[0:64, bank, :]))
                    if dbg_stop == "only_kv_a":
                        S.barrier()
                        return
                    bst = {}

                    def bc_mm(j):
                        bank = bank4()

                        def mm_b():
                            for kk in range(8):
                                ins = nc.tensor.matmul(ps[:, bank, 0:416], lhsT=hT[:, kk, j * 128:(j + 1) * 128],
                                                       rhs=wkv[:, kk, 128:544], start=(kk == 0), stop=(kk == 7))
                            return ins
                        S.op(S.pe, [wkvb, hTb], [pb[bank]], mm_b)
                        bst[j] = bank

                    def bc_chain(j):
                        sub = tq * 4 + j
                        bank = bst[j]
                        S.op(S.dve, [pb[bank], onesb], [ob],
                             lambda: nc.vector.tensor_copy(out=va_sb[:, ok, :, j, 0:64],
                                                           in_=ps[:, bank, 0:128].rearrange("p (k d) -> p k d", d=64)))
                        kk_, kb_ = krtb.next()
                        x1 = ps[:, bank, 384:400]
                        x2 = ps[:, bank, 400:416]
                        cos = cs[:, sub, 0:16]
                        sin = cs[:, sub, 16:32]

                        def rope():
                            nc.vector.tensor_tensor(out=rtmp[:, 0, :], in0=x1, in1=cos, op=ALU.mult)
                            nc.vector.tensor_tensor(out=rtmp[:, 1, :], in0=x2, in1=sin, op=ALU.mult)
                            nc.vector.tensor_tensor(out=rtmp[:, 2, :], in0=x1, in1=sin, op=ALU.mult)
                            return nc.vector.tensor_tensor(out=rtmp[:, 3, :], in0=x2, in1=cos, op=ALU.mult)
                        S.op(S.dve, [pb[bank], csb], [rtb], rope)

                        def rope2():
                            nc.vector.tensor_tensor(out=krt[:, kk_, 64:80], in0=rtmp[:, 0, :], in1=rtmp[:, 1, :],
                                                    op=ALU.subtract)
                            return nc.vector.tensor_tensor(out=krt[:, kk_, 80:96], in0=rtmp[:, 2, :], in1=rtmp[:, 3, :],
                                                           op=ALU.add)
                        S.op(S.dve, [rtb, onesb], [kb_], rope2)
                        c = 6 + j
                        S.op(S.act, [pb[bank]], [statb[c]] + R["junkb"],
                             lambda: nc.scalar.activation(out=junk[:, 0:256], in_=ps[:, bank, 128:384], func=AF.Square,
                                                          accum_out=stat[:, c:c + 1]))
                        rstd_ops(stat[:, c:c + 1], stat[:, c + 4:c + 5], statb[c], statb[c + 4], 256)
                        ck, cb = ckvnb.next()
                        fk, fb = ckvf_rr.next()
                        S.op(S.act, [pb[bank], statb[c + 4]], [fb],
                             lambda: nc.scalar.activation(out=ckvf[:, fk, :], in_=ps[:, bank, 128:384], func=AF.Copy,
                                                          scale=stat[:, c + 4:c + 5]))
                        S.op(S.dve, [fb, kvgb], [cb],
                             lambda: nc.vector.tensor_tensor(out=ckvn[:, ck, :], in0=ckvf[:, fk, :], in1=kvg[:, :],
                                                             op=ALU.mult))
                        bst[("c", j)] = (ck, cb, kk_, kb_)

                    def bc_tr(j):
                        ck, cb, kk_, kb_ = bst[("c", j)]
                        tb = 6 + (j % 2)
                        pst = ps[:, tb, :].bitcast(BF16)

                        def tr2():
                            for kc in range(2):
                                nc.tensor.transpose(out=pst[:, kc * 128:(kc + 1) * 128],
                                                    in_=ckvn[:, ck, kc * 128:(kc + 1) * 128], identity=ident[:])
                            return nc.tensor.transpose(out=pst[0:96, 256:384], in_=krt[:, kk_, :], identity=ident[:])
                        S.op(S.pe, [cb, kb_, identb], [pb[tb]], tr2)
                        S.op(S.act, [pb[tb]], [ckvnTb],
                             lambda: nc.scalar.copy(out=ckvnT[:, :, j * 128:(j + 1) * 128],
                                                    in_=pst[:, 0:256].rearrange("p (k t) -> p k t", k=2)))
                        S.op(S.act, [pb[tb]], [ob],
                             lambda: nc.scalar.copy(
                                 out=kbT_sb[64:96, ok, :, j * 128:(j + 1) * 128],
                                 in_=pst[64:96, 256:384].unsqueeze(1).to_broadcast([32, 8, 128])))
                    nxt = tq + 1 < NT // 512
                    nhT = hT2[:, (tq + 1) % 2, :, :]
                    nhTb = hTbs[(tq + 1) % 2]
                    fr = frs.pop(tq + 1, {})
                    if nxt and 0 not in fr:
                        fr[0] = nt_front(R, (tq + 1) * 4 + 0, y, yb)
                        fr[1] = nt_front(R, (tq + 1) * 4 + 1, y, yb)
                    for j in range(4):
                        bc_mm(j)
                    for j in range(4):
                        bc_chain(j)
                    for j in range(4):
                        bc_tr(j)
                        if nxt:
                            nt_back(R, fr.pop(j), nhT, nhTb, j * 128, eng="dve")
                            if j + 2 < 4:
                                fr[j + 2] = nt_front(R, (tq + 1) * 4 + j + 2, y, yb)
                    if tq + 2 < NT // 512:
                        frs[tq + 2] = {0: nt_front(R, (tq + 2) * 4 + 0, y, yb), 1: nt_front(R, (tq + 2) * 4 + 1, y, yb)}
                    for h in range(8):
                        bank = bank4()

                        def mm_k():
                            for kc in range(2):
                                ins = nc.tensor.matmul(ps[0:64, bank, :], lhsT=wukv[:, kc, h * 64:(h + 1) * 64],
                                                       rhs=ckvnT[:, kc, :], start=(kc == 0), stop=(kc == 1))
                            return ins
                        S.op(S.pe, [wukvb, ckvnTb], [pb[bank]], mm_k)
                        if h % 2 == 0:
                            S.op(S.act, [pb[bank]], [ob],
                                 lambda: nc.scalar.copy(out=kbT_sb[0:64, ok, h, :], in_=ps[0:64, bank, :]))
                        else:
                            S.op(S.dve, [pb[bank]], [ob],
                                 lambda: nc.vector.tensor_copy(out=kbT_sb[0:64, ok, h, :], in_=ps[0:64, bank, :]))
                    for j in range(4):
                        bank = bank4()

                        def mm_v():
                            for kc in range(2):
                                ins = nc.tensor.matmul(ps[:, bank, :], lhsT=ckvnT[:, kc, j * 128:(j + 1) * 128],
                                                       rhs=wukv[:, kc, 512:1024], start=(kc == 0), stop=(kc == 1))
                            return ins
                        S.op(S.pe, [wukvb, ckvnTb], [pb[bank]], mm_v)
                        cpe = S.act if j % 2 == 0 else S.dve

                        def cpv():
                            o = vb_sb[:, ok, :, j, 0:64]
                            i_ = ps[:, bank, :].rearrange("p (h d) -> p h d", d=64)
                            if cpe is S.act:
                                return nc.scalar.copy(out=o, in_=i_)
                            return nc.vector.tensor_copy(out=o, in_=i_)
                        S.op(cpe, [pb[bank], onesb], [ob], cpv)
                    if dbg_stop == "only_kv_d":
                        S.barrier()
                        return
                    t0, t1 = tq * 512, (tq + 1) * 512
                    S.dma(S.sp, [ob], [kvd_b[tq]], st_sems[ok][0], kaT_d[:, :, t0:t1], kaT_sb[:, ok, :, :])
                    S.dma(S.sp, [ob], [kvd_b[tq]], st_sems[ok][1], va_d[:, :, tq * 4:(tq + 1) * 4, :], va_sb[:, ok, :, :, :])
                    S.dma(S.sp, [ob], [kvd_b[tq]], st_sems[ok][2], kbT_d[:, :, t0:t1].rearrange("h r t -> r h t"),
                          kbT_sb[:, ok, :, :])
                    S.dma(S.sp, [ob], [kvd_b[tq]], st_sems[ok][3], vb_d[:, :, tq * 4:(tq + 1) * 4, :], vb_sb[:, ok, :, :, :])
                S.barrier()

        def mix_phase(l):
            win = W["w_in"][l].rearrange("(k p) c -> p k c", p=128)
            NQT = NT // 512
            with ExitStack() as st:
                hT2 = SB(st, "hT", [128, 2, 8, 512], BF16)
                xin = SB(st, "xin", [128, 2, D], F32)
                hn = SB(st, "hn", [128, 2, D], BF16)
                stat = SB(st, "stat", [128, 32], F32)
                gcol = SB(st, "gcol", [128, 8], F32)
                qaT = SB(st, "qaT", [64, 8, 512], BF16)
                cqn = SB(st, "cqn", [128, 2, 384], BF16)
                cqnT = SB(st, "cqnT", [128, 3, 512], BF16)
                cqf = SB(st, "cqf", [128, 1, 384], F32)
                cqf_rr = RR("cqf", 1)
                qtm = SB(st, "qtm", [128, 2, 8, 96], BF16)
                rtmp = SB(st, "rtmp", [128, 4, 4, 16], F32)
                qbT = SB(st, "qbT", [96, 8, 512], BF16)
                kaw = SB(st, "kaw", [64, 2, 2, 768], BF16)
                vaw = SB(st, "vaw", [128, 2, 2, 6 * 65 + 64], BF16)
                EB = SB(st, "EB", [128, 3, 8, 128], BF16)
                Ef = SB(st, "Ef", [128, 2, 512], F32)
                junk = Ef[:, 0, :].bitcast(BF16)
                PTb = SB(st, "PT", [128, 6, 512], BF16)
                PTa = PTb
                osb = SB(st, "osb", [65, 3, 512], F32)
                oaT = SB(st, "oaT", [64, 8, 512], BF16)
                obT = SB(st, "obT", [64, 8, 512], BF16)
                kbh = SB(st, "kbh", [96, 2, NT], BF16)
                vbh = SB(st, "vbh", [128, 2, NSUB * 65 + 64], BF16)
                mT = SB(st, "mT", [128, 8, 512], BF16)
                sga = SB(st, "sga", [128, 512], F32)
                sgb = SB(st, "sgb", [128, 512], F32)
                ybuf = SB(st, "ybuf", [128, 2, D], F32)
                gpost = SB(st, "gpost", [128, D], F32)
                qg = SB(st, "qg", [128, 384], F32)
                cs = SB(st, "cs", [128, NSUB, 32], F32)
                esk = SB(st, "esk", [65, 8], F32)
                onesf = SB(st, "onesf", [65, 64], F32)
                flags = SB(st, "flags", [128, 2], F32)
                wqa = SB(st, "wqa", [128, 8, 512], BF16)
                wcq = SB(st, "wcq", [128, 8, 384], BF16)
                wuq = SB(st, "wuq", [128, 3, 768], BF16)
                wab = SB(st, "wab", [64, 2, 2, 8, 128], BF16)
                wg = SB(st, "wg", [128, 2, 2, 8, 128], BF16)
                wo = SB(st, "wo", [128, 8, D], BF16)

                R = dict(xin=xin, xin_rr=RR("xin", 2), xin_sem=[S.dsem(f"xin{i}") for i in range(2)], stat=stat,
                         statb=[Buf(f"stat{i}") for i in range(32)], hn=hn, hn_rr=RR("hn", 2), gcol=gcol,
                         gcolb=Buf("gcol"), junk=junk[:, :], junkb=[Buf("junk")])
                statb = R["statb"]
                S.dma(S.sp, [], [R["gcolb"]], S.dsem("gcol"), gcol[:, :],
                      W["mix_pre_g"][l].rearrange("(k p) -> p k", p=128), allow_slow_non_contiguous=True)
                gpostb, qgb, csb, eskb, onesb, flagb, EBb = (Buf(n) for n in ("gpost", "qg", "cs", "esk", "ones", "flag", "EB"))
                S.dma(S.sp, [], [gpostb], S.dsem("gpost"), gpost[:, :], W["mix_post_g"][l:l + 1, :].to_broadcast([128, D]))
                S.dma(S.sp, [], [qgb], S.dsem("qg"), qg[:, :], W["q_norm_g"][l:l + 1, :].to_broadcast([128, 384]))
                S.dma(S.sp, [], [csb], S.dsem("cs"), cs[:, :, :], C["c_rope"].rearrange("(s p) e -> p s e", p=128))
                S.dma(S.sp, [], [eskb], S.dsem("esk"), esk[64:65, :], W["sink"][l:l + 1, :])
                S.op(S.act, [eskb], [eskb], lambda: nc.scalar.activation(out=esk[64:65, :], in_=esk[64:65, :], func=AF.Exp))
                S.op(S.pool, [], [onesb], lambda: nc.gpsimd.memset(onesf[:], 1.0))
                padb = Buf("pad")

                def pad_ms():
                    nc.gpsimd.memset(vaw[:], 0.0)
                    return nc.gpsimd.memset(vbh[:], 0.0)
                S.op(S.pool, [], [padb], pad_ms)
                S.dma(S.sp, [], [flagb], S.dsem("flag"), flags[:, :], C["c_flags"][:, :])
                mTb = Buf("mT")
                ebsem = S.dsem("EB")
                ebtmp = mT[:, :, :].rearrange("p a b -> p (a b)")[:, 0:3072].rearrange("p (c h q) -> p c h q", c=3, h=8)
                for c in range(3):
                    for h in range(8):
                        src = bass.AP(tensor=ev_d.tensor, offset=h * 640 + 128 * c + 1, ap=[[1, 128], [1, 128]])
                        S.dma(S.sp, [ebd_b], [mTb], ebsem, ebtmp[:, c, h, :], src)
                for c in range(3):
                    e0 = ebtmp[:, c, :, :]
                    rv = bass.AP(tensor=e0.tensor, offset=e0.offset + 127, ap=[list(e0.ap[0]), [128, 8], [-1, 128]])
                    S.op(S.dve, [mTb], [EBb], lambda: nc.vector.tensor_copy(out=EB[:, c, :, :], in_=rv))
                wqab, wcqb, wuqb, wob = (Buf(n) for n in ("wqa", "wcq", "wuq", "wo"))
                S.dma(S.pool, [], [wqab], S.dsem("wqa", sw=True), wqa[:, :, :], win[:, :, 0:512])
                S.dma(S.pool, [], [wcqb], S.dsem("wcq", sw=True), wcq[:, :, :], win[:, :, 768:1152])
                S.dma(S.pool, [], [wuqb], S.dsem("wuq", sw=True), wuq[:, :, :], W["w_uq"][l].rearrange("(k p) c -> p k c", p=128))
                wov = W["w_o"][l].rearrange("(k p) c -> p k c", p=128)
                wosem = S.dsem("wo", sw=True)
                for kk in range(8):
                    S.dma(S.pool, [], [wob], wosem, wo[:, kk, :], wov[:, kk, :])

                dc_rr = RR("wdc", 2)
                dc_sem = [S.dsem("wdc0", sw=True), S.dsem("wdc1", sw=True)]
                dc_state = {"next": 0, "slots": {}}

                def issue_dc(upto):
                    while dc_state["next"] <= min(upto, NQT * 8 - 1):
                        g = dc_state["next"]
                        dc = g % 8
                        k, b = dc_rr.next()
                        S.dma(S.pool, [wcv_b[l]], [b], dc_sem[k], wab[:, k, :, :, :], wab_d[l, dc, :, :, :, :])
                        S.dma(S.pool, [wcv_b[l]], [b], dc_sem[k], wg[:, k, :, :, :], wg_d[l, dc, :, :, :, :])
                        dc_state["slots"][g] = (k, b)
                        dc_state["next"] += 1

                kv_rr = RR("kvh", 2)
                kv_sem = [S.dsem("kvh0"), S.dsem("kvh1")]
                kv_state = {"next": 0, "slots": {}}

                def issue_kvh(upto):
                    while kv_state["next"] <= min(upto, NQT * 8 - 1):
                        g = kv_state["next"]
                        h = g % 8
                        k, b = kv_rr.next()
                        S.dma(S.sp, kvd_b + [padb], [b], kv_sem[k], kbh[:, k, :], kbT_d[h, :, :])
                        S.dma(S.sp, kvd_b, [b], kv_sem[k], vbh[:, k, 0:NSUB * 65], vb_d[:, h, :, :].rearrange("p c e -> p (c e)"))
                        xlo = 16 if (g // 8) // 4 == 0 else 0
                        S.op(S.dve, [b, flagb], [b],
                             lambda: nc.vector.tensor_scalar(out=vbh[:, k, xlo * 65:(xlo + 16) * 65],
                                                             in0=vbh[:, k, xlo * 65:(xlo + 16) * 65],
                                                             scalar1=flags[:, 0:1], scalar2=None, op0=ALU.mult))
                        kv_state["slots"][g] = (k, b)
                        kv_state["next"] += 1

                hTbs = [Buf("hT0"), Buf("hT1")]
                qaTb, cqnTb, qbTb, oaTb, obTb = (Buf(n) for n in ("qaT", "cqnT", "qbT", "oaT", "obT"))
                cqn_rr = RR("cqn", 2)
                qtm_rr = RR("qtm", 2)
                rtb = Buf("rtmp")
                kw_rr = RR("kw", 2)
                kw_sem = [S.dsem("kw0"), S.dsem("kw1")]
                ef_rr = RR("Ef", 2)
                R["junkb"] = [ef_rr.bufs[0]]
                pta_rr = RR("PTa", 3)
                ptb_bufs = [Buf(f"PTb{i}") for i in range(6)]
                pta_rr.bufs = ptb_bufs[0:3]
                osb_rr = RR("osb", 3)
                sgab, sgbb, m1b = Buf("sga"), Buf("sgb"), Buf("m1")
                y_rr = RR("ybuf", 2)
                y_sem = [S.dsem("yst0"), S.dsem("yst1")]
                cnt = {"b4": 0, "ob": 0, "bc": 0, "yb": 0, "pair": 0}

                def bank4():
                    b = cnt["b4"] % 4
                    cnt["b4"] += 1
                    return b

                def norm_a(obank, sink_kv):
                    ok, ob_ = osb_rr.next()
                    if sink_kv is None:
                        S.op(S.dve, [pb[obank]], [ob_], lambda: nc.vector.tensor_copy(out=osb[:, ok, :], in_=ps[0:65, obank, :]))
                    else:
                        S.op(S.act, [pb[obank]], [ob_], lambda: nc.scalar.copy(out=osb[:, ok, :], in_=ps[0:65, obank, :]))

                    if sink_kv is not None:
                        S.op(S.dve, [ob_, eskb], [ob_], lambda: nc.vector.tensor_tensor(
                            out=osb[64:65, ok, :].rearrange("p (g q) -> p g q", g=4),
                            in0=osb[64:65, ok, :].rearrange("p (g q) -> p g q", g=4),
                            in1=esk[64:65, sink_kv * 4:(sink_kv + 1) * 4].unsqueeze(2).to_broadcast([1, 4, 128]),
                            op=ALU.add))
                        S.op(S.act, [ob_], [ob_],
                             lambda: nc.scalar.activation(out=osb[64:65, ok, :], in_=osb[64:65, ok, :], func=AF.Ln))
                        S.op(S.act, [ob_], [ob_],
                             lambda: nc.scalar.activation(out=osb[64:65, ok, :], in_=osb[64:65, ok, :], func=AF.Exp,
                                                          scale=-1.0))
                    else:
                        S.op(S.dve, [ob_], [ob_],
                             lambda: nc.vector.reciprocal(out=osb[64:65, ok, :], in_=osb[64:65, ok, :]))
                    return ok, ob_

                def norm_b(ok, ob_, dest, destb, bank=None):
                    if bank is None:
                        bcb = 6 + (cnt["bc"] % 2)
                        cnt["bc"] += 1
                    else:
                        bcb = bank
                    S.op(S.pe, [ob_, onesb], [pb[bcb]],
                         lambda: nc.tensor.matmul(ps[0:64, bcb, :], lhsT=onesf[64:65, 0:64], rhs=osb[64:65, ok, :],
                                                  start=True, stop=True))
                    S.op(S.dve, [ob_, pb[bcb]], [destb],
                         lambda: nc.vector.tensor_tensor(out=dest, in0=osb[0:64, ok, :], in1=ps[0:64, bcb, :], op=ALU.mult))

                issue_kvh(1)
                qst = {}

                def s123(qt):
                    hT = hT2[:, qt % 2, :, :]
                    hTb = hTbs[qt % 2]
                    f0 = nt_front(R, qt * 4 + 0, y, yb)
                    f1 = nt_front(R, qt * 4 + 1, y, yb)
                    yield
                    nt_back(R, f0, hT, hTb, 0)
                    f2 = nt_front(R, qt * 4 + 2, y, yb)
                    yield
                    nt_back(R, f1, hT, hTb, 128)
                    f3 = nt_front(R, qt * 4 + 3, y, yb)
                    yield
                    nt_back(R, f2, hT, hTb, 256)
                    yield
                    nt_back(R, f3, hT, hTb, 384)
                    lo, hi = max(0, 4 * qt - 1), min(NSUB, 4 * qt + 5)
                    b0 = 4 * qt - 1
                    wk, wkb = kw_rr.next()
                    rtiles = kvd_b[lo * 128 // 512:(hi * 128 - 1) // 512 + 1]
                    S.dma(S.sp, rtiles + [padb], [wkb], kw_sem[wk], kaw[:, wk, :, (lo - b0) * 128:(hi - b0) * 128],
                          kaT_d[:, :, lo * 128:hi * 128])
                    S.dma(S.sp, rtiles, [wkb], kw_sem[wk],
                          vaw[:, wk, :, 0:390].rearrange("p k (b e) -> p k b e", e=65)[:, :, lo - b0:hi - b0, :],
                          va_d[:, :, lo:hi, :])
                    qst[qt] = (b0, wk, wkb)
                    for h in range(8):
                        bank = bank4()

                        def mm_qa():
                            for kk in range(8):
                                ins = nc.tensor.matmul(ps[0:64, bank, :], lhsT=wqa[:, kk, h * 64:(h + 1) * 64],
                                                       rhs=hT[:, kk, :], start=(kk == 0), stop=(kk == 7))
                            return ins
                        S.op(S.pe, [wqab, hTb], [pb[bank]], mm_qa)
                        if h % 2 == 0:
                            S.op(S.act, [pb[bank]], [qaTb],
                                 lambda: nc.scalar.activation(out=qaT[:, h, :], in_=ps[0:64, bank, :], func=AF.Copy, scale=0.125))
                        else:
                            S.op(S.dve, [pb[bank]], [qaTb],
                                 lambda: nc.vector.tensor_scalar(out=qaT[:, h, :], in0=ps[0:64, bank, :], scalar1=0.125,
                                                                 scalar2=None, op0=ALU.mult))
                        if h % 4 == 3:
                            yield
                    cqst = {}

                    def cq_mm(j):
                        bank = bank4()

                        def mm_cq():
                            for kk in range(8):
                                ins = nc.tensor.matmul(ps[:, bank, 0:384], lhsT=hT[:, kk, j * 128:(j + 1) * 128],
                                                       rhs=wcq[:, kk, :], start=(kk == 0), stop=(kk == 7))
                            return ins
                        S.op(S.pe, [wcqb, hTb], [pb[bank]], mm_cq)
                        cqst[j] = bank

                    def cq_chain(j):
                        bank = cqst[j]
                        c = 8 + (j % 2)
                        S.op(S.act, [pb[bank]], [statb[c]] + R["junkb"],
                             lambda: nc.scalar.activation(out=junk[:, 0:384], in_=ps[:, bank, 0:384], func=AF.Square,
                                                          accum_out=stat[:, c:c + 1]))
                        rstd_ops(stat[:, c:c + 1], stat[:, c + 2:c + 3], statb[c], statb[c + 2], 384)
                        ck, cb = cqn_rr.next()
                        fk, fb = cqf_rr.next()
                        S.op(S.act, [pb[bank], statb[c + 2]], [fb],
                             lambda: nc.scalar.activation(out=cqf[:, fk, :], in_=ps[:, bank, 0:384], func=AF.Copy,
                                                          scale=stat[:, c + 2:c + 3]))
                        S.op(S.dve, [fb, qgb], [cb],
                             lambda: nc.vector.tensor_tensor(out=cqn[:, ck, :], in0=cqf[:, fk, :], in1=qg[:, :],
                                                             op=ALU.mult))
                        cqst[("c", j)] = (ck, cb)

                    def cq_tr(j):
                        ck, cb = cqst[("c", j)]
                        tb = 6 + (j % 2)
                        pst = ps[:, tb, :].bitcast(BF16)

                        def tr3():
                            for kc in range(3):
                                ins = nc.tensor.transpose(out=pst[:, kc * 128:(kc + 1) * 128],
                                                          in_=cqn[:, ck, kc * 128:(kc + 1) * 128], identity=ident[:])
                            return ins
                        S.op(S.pe, [cb, identb], [pb[tb]], tr3)
                        S.op(S.act, [pb[tb]], [cqnTb],
                             lambda: nc.scalar.copy(out=cqnT[:, :, j * 128:(j + 1) * 128],
                                                    in_=pst[:, 0:384].rearrange("p (k t) -> p k t", k=3)))
                    cq_mm(0)
                    cq_mm(1)
                    cq_chain(0)
                    cq_mm(2)
                    cq_chain(1)
                    cq_mm(3)
                    yield
                    cq_tr(0)
                    cq_chain(2)
                    cq_tr(1)
                    cq_chain(3)
                    yield
                    cq_tr(2)
                    cq_tr(3)

                    def q_mm(j):
                        sub = qt * 4 + j
                        qk, qb_ = qtm_rr.next()
                        cqst[("q", j)] = (qk, qb_)
                        for half in range(2):
                            bank = bank4()

                            def mm_q():
                                for kc in range(3):
                                    ins = nc.tensor.matmul(ps[:, bank, 0:384], lhsT=cqnT[:, kc, j * 128:(j + 1) * 128],
                                                           rhs=wuq[:, kc, half * 384:(half + 1) * 384],
                                                           start=(kc == 0), stop=(kc == 2))
                                return ins
                            S.op(S.pe, [wuqb, cqnTb], [pb[bank]], mm_q)
                            psv = ps[:, bank, 0:384].rearrange("p (h e) -> p h e", h=4)
                            x1 = psv[:, :, 64:80]
                            x2 = psv[:, :, 80:96]
                            cos = cs[:, sub, 0:16].unsqueeze(1).to_broadcast([128, 4, 16])
                            sin = cs[:, sub, 16:32].unsqueeze(1).to_broadcast([128, 4, 16])

                            def rope():
                                nc.vector.tensor_copy(out=qtm[:, qk, half * 4:(half + 1) * 4, 0:64], in_=psv[:, :, 0:64])
                                nc.vector.tensor_tensor(out=rtmp[:, 0, :, :], in0=x1, in1=cos, op=ALU.mult)
                                nc.vector.tensor_tensor(out=rtmp[:, 1, :, :], in0=x2, in1=sin, op=ALU.mult)
                                nc.vector.tensor_tensor(out=rtmp[:, 2, :, :], in0=x1, in1=sin, op=ALU.mult)
                                return nc.vector.tensor_tensor(out=rtmp[:, 3, :, :], in0=x2, in1=cos, op=ALU.mult)
                            S.op(S.dve, [pb[bank], csb], [rtb, qb_], rope)

                            def rope2():
                                nc.vector.tensor_tensor(out=qtm[:, qk, half * 4:(half + 1) * 4, 64:80], in0=rtmp[:, 0, :, :],
                                                        in1=rtmp[:, 1, :, :], op=ALU.subtract)
                                return nc.vector.tensor_tensor(out=qtm[:, qk, half * 4:(half + 1) * 4, 80:96],
                                                               in0=rtmp[:, 2, :, :], in1=rtmp[:, 3, :, :], op=ALU.add)
                            S.op(S.dve, [rtb], [qb_], rope2)

                    def q_tr(j):
                        qk, qb_ = cqst[("q", j)]
                        tb = 6 + (j % 2)
                        pst = ps[:, tb, :].bitcast(BF16)

                        def tr4():
                            for h in range(8):
                                ins = nc.tensor.transpose(out=pst[0:96, h * 128:(h + 1) * 128], in_=qtm[:, qk, h, :],
                                                          identity=ident[:])
                            return ins
                        S.op(S.pe, [qb_, identb], [pb[tb]], tr4)
                        S.op(S.act, [pb[tb]], [qbTb], lambda: nc.scalar.copy(
                            out=qbT[:, :, j * 128:(j + 1) * 128], in_=pst[0:96, :].rearrange("p (h t) -> p h t", h=8)))
                    q_mm(0)
                    q_mm(1)
                    yield
                    q_tr(0)
                    q_mm(2)
                    q_tr(1)
                    q_mm(3)
                    yield
                    q_tr(2)
                    q_tr(3)
                    yield

                def s45(qt, g7):
                    b0, wk, wkb = qst[qt]
                    wsteps = []
                    for j in range(4):
                        n = 4 * qt + j
                        for kv in range(2):
                            chunks = [c for c in range(3) if 0 <= n - 1 + c < NSUB]
                            for ci, c in enumerate(chunks):
                                wsteps.append((j, kv, ci, c, len(chunks)))
                    LAW = 2
                    winfo = {}
                    wobank = {}
                    pending = []

                    def flush(upto_i, keep=0):
                        while len(pending) > keep or (pending and pending[0][0] <= upto_i):
                            it_ = pending.pop(0)
                            norm_b(it_[1], it_[2], it_[3], it_[4], bank=(it_[5] if len(it_) > 5 else None))
                    for i in range(len(wsteps) + LAW):
                        if i < len(wsteps):
                            j, kv, ci, c, nch = wsteps[i]
                            n = 4 * qt + j
                            kb_ = n - 1 + c
                            bi = kb_ - b0
                            if ci == 0:
                                wobank[(j, kv)] = 4 + (cnt["ob"] % 2)
                                cnt["ob"] += 1
                            bank = cnt["b4"] % 2
                            cnt["b4"] += 1
                            next(g7, None)
                            S.op(S.pe, [wkb, qaTb], [pb[bank]],
                                 lambda: nc.tensor.matmul(ps[:, bank, :], lhsT=kaw[:, wk, kv, bi * 128:(bi + 1) * 128],
                                                          rhs=qaT[:, kv * 4:(kv + 1) * 4, j * 128:(j + 1) * 128],
                                                          start=True, stop=True))
                            ek, eb_ = ef_rr.next()
                            S.op(S.act, [pb[bank]], [eb_],
                                 lambda: nc.scalar.activation(out=Ef[:, ek, :], in_=ps[:, bank, :], func=AF.Exp))
                            pk, pbuf = pta_rr.next()
                            cross = (kb_ // 16) != (n // 16)
                            ebv = EB[:, c, kv * 4:(kv + 1) * 4, :]
                            efv = Ef[:, ek, :].rearrange("p (g q) -> p g q", g=4)
                            ptv = PTa[:, pk, :].rearrange("p (g q) -> p g q", g=4)
                            if cross:
                                S.op(S.dve, [eb_, EBb, flagb], [pbuf],
                                     lambda: nc.vector.scalar_tensor_tensor(out=ptv, in0=efv, scalar=flags[:, 0:1],
                                                                            in1=ebv, op0=ALU.mult, op1=ALU.mult))
                            else:
                                S.op(S.dve, [eb_, EBb], [pbuf],
                                     lambda: nc.vector.tensor_tensor(out=ptv, in0=efv, in1=ebv, op=ALU.mult))
                            winfo[i] = (pk, pbuf, bi)
                        ip = i - LAW
                        if ip >= 0:
                            j, kv, ci, c, nch = wsteps[ip]
                            pk, pbuf, bi = winfo.pop(ip)
                            obank = wobank[(j, kv)]
                            S.op(S.pe, [wkb, pbuf], [pb[obank]],
                                 lambda: nc.tensor.matmul(ps[:, obank, :], lhsT=vaw[:, wk, kv, bi * 65:bi * 65 + 128],
                                                          rhs=PTa[:, pk, :], start=(ci == 0), stop=(ci == nch - 1)))
                            if ci == nch - 1:
                                flush(i, keep=2)
                                ok, ob_ = norm_a(obank, kv)
                                pending.append((i + 7, ok, ob_, oaT[:, kv * 4:(kv + 1) * 4, j * 128:(j + 1) * 128], oaTb))
                        flush(i, keep=3)
                    flush(10 ** 9, keep=0)
                    for _ in g7:
                        pass
                    issue_dc(qt * 8)
                    steps = [(h, kp) for h in range(8) for kp in range(NSUB // 2)]
                    LA = 2
                    sc_ = (64 + 32) ** -0.5
                    info = {}
                    hstate = {}
                    for i in range(len(steps) + LA):
                        if i < len(steps):
                            h, kp = steps[i]
                            g = qt * 8 + h
                            if kp == LA + 1:
                                issue_kvh(g + 1)
                            if kp == 0:
                                obank = 6 + (h % 2)
                                hstate[h] = obank
                            hk, hb_ = kv_state["slots"][g]
                            bp = 2 * (cnt["pair"] % 3)
                            pp = 2 * (cnt["pair"] % 3)
                            cnt["pair"] += 1

                            def qk2():
                                for t in range(2):
                                    kc = 2 * kp + t
                                    ins = nc.tensor.matmul(ps[:, bp + t, :], lhsT=kbh[:, hk, kc * 128:(kc + 1) * 128],
                                                           rhs=qbT[:, h, :], start=True, stop=True)
                                return ins
                            S.op(S.pe, [hb_, qbTb], [pb[bp], pb[bp + 1]], qk2)
                            pbufs = [ptb_bufs[pp], ptb_bufs[pp + 1]]
                            S.op(S.act, [pb[bp], pb[bp + 1]], pbufs,
                                 lambda: nc.scalar.activation(out=PTb[:, pp:pp + 2, :], in_=ps[:, bp:bp + 2, :], func=AF.Exp,
                                                              scale=sc_))
                            info[i] = (pp, pbufs)
                        ip = i - LA
                        if ip >= 0:
                            h, kp = steps[ip]
                            g = qt * 8 + h
                            hk, hb_ = kv_state["slots"][g]
                            pp, pbufs = info.pop(ip)
                            obank = hstate[h]

                            def pv2():
                                for t in range(2):
                                    kc = 2 * kp + t
                                    ins = nc.tensor.matmul(ps[:, obank, :], lhsT=vbh[:, hk, kc * 65:kc * 65 + 128],
                                                           rhs=PTb[:, pp + t, :], start=(kc == 0), stop=(kc == NSUB - 1))
                                return ins
                            S.op(S.pe, [hb_] + pbufs, [pb[obank]], pv2)
                            if kp == NSUB // 2 - 1:
                                flush(i, keep=2)
                                ok, ob_ = norm_a(obank, None)
                                pending.append((i + 5, ok, ob_, obT[:, h, :], obTb, obank))
                        flush(i, keep=3)
                    flush(10 ** 9, keep=0)

                def s6(qt, gen):
                    hT = hT2[:, qt % 2, :, :]
                    hTb = hTbs[qt % 2]
                    for dc in range(8):
                        g = qt * 8 + dc
                        issue_dc(g + 1)
                        wk_, wdb_ = dc_state["slots"][g]
                        base = 4 * (dc % 2)

                        def mm_o(wt, which, src, bank):
                            for h in range(8):
                                ins = nc.tensor.matmul(ps[:, bank, :], lhsT=wt[:, wk_, which, h, :], rhs=src[:, h, :],
                                                       start=(h == 0), stop=(h == 7))
                            return ins
                        S.op(S.pe, [wdb_, hTb], [pb[base + 2]], lambda: mm_o(wg, 0, hT, base + 2))
                        S.op(S.pe, [wdb_, hTb], [pb[base + 3]], lambda: mm_o(wg, 1, hT, base + 3))
                        S.op(S.pe, [wdb_, oaTb], [pb[base]], lambda: mm_o(wab, 0, oaT, base))
                        S.op(S.pe, [wdb_, obTb], [pb[base + 1]], lambda: mm_o(wab, 1, obT, base + 1))
                        S.op(S.act, [pb[base + 2]], [sgab],
                             lambda: nc.scalar.activation(out=sga[:, :], in_=ps[:, base + 2, :], func=AF.Sigmoid))
                        S.op(S.act, [pb[base + 3]], [sgbb],
                             lambda: nc.scalar.activation(out=sgb[:, :], in_=ps[:, base + 3, :], func=AF.Sigmoid))
                        S.op(S.dve, [sgab, pb[base]], [sgab],
                             lambda: nc.vector.tensor_tensor(out=sga[:, :], in0=sga[:, :], in1=ps[:, base, :], op=ALU.mult))
                        S.op(S.dve, [sgbb, pb[base + 1]], [sgbb],
                             lambda: nc.vector.tensor_tensor(out=sgb[:, :], in0=sgb[:, :], in1=ps[:, base + 1, :], op=ALU.mult))
                        S.op(S.dve, [sgab, sgbb], [mTb],
                             lambda: nc.vector.tensor_tensor(out=mT[:, dc, :], in0=sga[:, :], in1=sgb[:, :], op=ALU.add))
                        next(gen, None)

                def s7(qt):
                    pend = {}

                    def ld(j):
                        if j >= 4:
                            return
                        sub = qt * 4 + j
                        k, xb = R["xin_rr"].next()
                        S.dma(S.sp, [yb[sub]], [xb], R["xin_sem"][k], xin[:, k, :], y[sub * 128:(sub + 1) * 128, :])
                        pend[j] = (k, xb)
                    ld(0)
                    for j in range(4):
                        sub = qt * 4 + j
                        ld(j + 1)
                        xk, xb = pend.pop(j)
                        yk, ybb = y_rr.next()
                        for half in range(2):
                            bank = 2 + cnt["yb"] % 2
                            cnt["yb"] += 1

                            def mm_wo():
                                for dc in range(8):
                                    ins = nc.tensor.matmul(ps[:, bank, :], lhsT=mT[:, dc, j * 128:(j + 1) * 128],
                                                           rhs=wo[:, dc, half * 512:(half + 1) * 512],
                                                           start=(dc == 0), stop=(dc == 7))
                                return ins
                            S.op(S.pe, [mTb, wob], [pb[bank]], mm_wo)
                            S.op(S.act, [pb[bank]], [ybb],
                                 lambda: nc.scalar.copy(out=ybuf[:, yk, half * 512:(half + 1) * 512], in_=ps[:, bank, :]))
                            c = 16 + yk * 2 + half
                            S.op(S.act, [pb[bank]], [statb[c]] + R["junkb"],
                                 lambda: nc.scalar.activation(out=junk[:, 0:512], in_=ps[:, bank, :], func=AF.Square,
                                                              accum_out=stat[:, c:c + 1]))
                            yield
                        c0 = 16 + yk * 2
                        c2 = 20 + yk
                        S.op(S.dve, [statb[c0], statb[c0 + 1]], [statb[c2]],
                             lambda: nc.vector.tensor_tensor(out=stat[:, c2:c2 + 1], in0=stat[:, c0:c0 + 1],
                                                             in1=stat[:, c0 + 1:c0 + 2], op=ALU.add))
                        rstd_ops(stat[:, c2:c2 + 1], stat[:, c2 + 2:c2 + 3], statb[c2], statb[c2 + 2], D)
                        S.op(S.dve, [ybb, statb[c2 + 2], gpostb], [ybb],
                             lambda: nc.vector.scalar_tensor_tensor(out=ybuf[:, yk, :], in0=ybuf[:, yk, :],
                                                                    scalar=stat[:, c2 + 2:c2 + 3], in1=gpost[:, :],
                                                                    op0=ALU.mult, op1=ALU.mult))
                        S.op(S.dve, [ybb, xb], [ybb],
                             lambda: nc.vector.tensor_tensor(out=ybuf[:, yk, :], in0=ybuf[:, yk, :],
                                                             in1=xin[:, xk, :], op=ALU.add))
                        S.dma(S.sp, [ybb], [yb[sub]], y_sem[yk], y[sub * 128:(sub + 1) * 128, :], ybuf[:, yk, :])
                        yield

                gen0 = s123(0)
                for _ in gen0:
                    pass
                g7 = iter(())
                for qt in range(NQT):
                    s45(qt, g7)
                    gen = s123(qt + 1) if qt + 1 < NQT else iter(())
                    s6(qt, gen)
                    for _ in gen:
                        pass
                    g7 = s7(qt)
                for _ in g7:
                    pass
                S.barrier()

        phases = [
            lambda: ffn_phase(0, "ffn1", x_in, xsrc0, y, yb),
            lambda: kv_phase(0),
            lambda: mix_phase(0),
            lambda: ffn_phase(0, "ffn2", y, yb, y, yb),
            lambda: ffn_phase(1, "ffn1", y, yb, y, yb),
            lambda: kv_phase(1),
            lambda: mix_phase(1),
            lambda: ffn_phase(1, "ffn2", y, yb, y, yb),
        ]
        if dbg_stop == "only_pro":
            prologue()
            phases = []
        if dbg_stop is not None and dbg_stop.startswith("only_kv"):
            kv_phase(0)
            phases = []
        if n_phases >= 3 and dbg_stop is None:
            prologue()
        S.barrier()
        S.release_phase()
        for i, ph in enumerate(phases):
            if i >= n_phases:
                break
            if ph is not None:
                ph()
                S.barrier()
                S.release_phase()
        S.barrier()
    return nc


def shard_tokens(x_prompt, x_sample):
    xs = []
    for c in range(N_CORES):
        if c < 4:
            xs.append(np.ascontiguousarray(x_sample[c]))
        else:
            j = 2 * (c - 4)
            xs.append(np.ascontiguousarray(np.concatenate([x_prompt[j], x_prompt[j + 1]], axis=0)))
    return xs


_HC = {}


def _t5_bucket_np(rel):
    import jax
    import jax.numpy as jnp
    with jax.default_device(jax.devices("cpu")[0]):
        rel = jnp.asarray(rel, dtype=jnp.int32)
        half = 16
        max_exact = 8
        ret = jnp.where(rel > 0, half, 0)
        n = jnp.abs(rel)
        nf = jnp.maximum(n, 1).astype(jnp.float32)
        large = max_exact + (jnp.log(nf / max_exact) / math.log(128 / max_exact) * (half - max_exact)).astype(jnp.int32)
        large = jnp.minimum(large, half - 1)
        return np.asarray(ret + jnp.where(n < max_exact, n, large))


def host_constants(core):
    if "ident" not in _HC:
        _HC["ident"] = np.eye(128, dtype=np.float32)
        rel = np.arange(640) - 256
        bucket = _t5_bucket_np(rel)
        inside = np.abs(rel) <= 128
        oh = np.zeros((33, 640), np.float32)
        for b in range(32):
            oh[b] = ((bucket == b) & inside).astype(np.float32)
        oh[32] = (~inside).astype(np.float32)
        _HC["onehot2"] = np.ascontiguousarray(oh)
        inv_freq = (10000.0 ** (-np.arange(0, 32, 2, dtype=np.float32) / np.float32(32))).astype(np.float32)
        for kind, seqlen in (("s", 4096), ("p", 2048)):
            pos = (np.arange(NT) % seqlen).astype(np.float32)
            ang = (pos[:, None] * inv_freq[None, :]).astype(np.float32)
            _HC["rope_" + kind] = np.ascontiguousarray(
                np.concatenate([np.cos(ang), np.sin(ang)], axis=1).astype(np.float32))
        fl = np.zeros((128, 2), np.float32)
        fl[:, 0] = 1.0
        _HC["flags_s"] = fl
        fl = np.zeros((128, 2), np.float32)
        fl[:, 1] = -30000.0
        _HC["flags_p"] = fl
    kind = "s" if core < 4 else "p"
    return {"c_ident": _HC["ident"], "c_onehot2": _HC["onehot2"], "c_rope": _HC["rope_" + kind],
            "c_flags": _HC["flags_" + kind]}


def kernel(**inputs):
    x_prompt = np.asarray(inputs["x_prompt"], dtype=np.float32)
    x_sample = np.asarray(inputs["x_sample"], dtype=np.float32)
    xs = shard_tokens(x_prompt, x_sample)
    nc = build_program()
    wmap = {n: np.ascontiguousarray(np.asarray(inputs[n], dtype=np.float32)) for n in nc.used_weights}
    in_maps = []
    for c in range(N_CORES):
        m = dict(wmap)
        m["x"] = xs[c]
        hc = host_constants(c)
        for n in nc.used_consts:
            m[n] = hc[n]
        in_maps.append(m)
    res = run_bass_kernel_spmd(nc, in_maps, core_ids=list(range(N_CORES)))
    ys = [np.asarray(r["y"]) for r in res.results]
    y_sample = np.stack(ys[0:4], axis=0)
    y_prompt = np.stack([ys[4 + j // 2][(j % 2) * 2048:(j % 2 + 1) * 2048] for j in range(8)], axis=0)
    return (y_prompt.astype(np.float32), y_sample.astype(np.float32))
```

```python
import math
from contextlib import ExitStack

import numpy as np
import concourse.bass as bass
import concourse.mybir as mybir
from concourse.bass_utils import run_bass_kernel_spmd

F32 = mybir.dt.float32
BF16 = mybir.dt.bfloat16
AF = mybir.ActivationFunctionType
ALU = mybir.AluOpType

D = 1024
NT = 4096
NSUB = NT // 128
DFF = 2816
NFC = DFF // 128
DEPTH = 2
EPS = 1e-6
IN_COLS = 3488
N_CORES = 8
SEM_ROT = 16000


class Buf:
    __slots__ = ("name", "w", "r", "rd", "excl")

    def __init__(self, name, excl=False):
        self.name = name
        self.excl = excl
        self.w = None
        self.r = {}
        self.rd = []


class DmaSem:
    def __init__(self, sched, name):
        self.h = sched.new_sem(name)
        self.cnt = 0


class EngQ:
    def __init__(self, sched, eng, name, is_pe=False, dma_only=False):
        self.sched = sched
        self.eng = eng
        self.name = name
        self.is_pe = is_pe
        self.cnt = 0
        self.sem = None
        self.nsem = 0
        self.seen = {}
        self.dma_only = dma_only

    def wait(self, tok, raw=True, rar=False):
        if tok is None:
            return
        sem, val, src = tok
        if src is self and (self.is_pe or rar):
            return
        key = id(sem)
        if self.seen.get(key, 0) >= val:
            return
        self.eng.wait_ge(sem, val)
        self.seen[key] = val

    def mark(self, ins):
        if self.sem is None or self.cnt >= SEM_ROT:
            self.sem = self.sched.new_sem(f"q_{self.name}_{self.nsem}")
            self.nsem += 1
            self.cnt = 0
        ins.then_inc(self.sem, 1)
        self.cnt += 1
        return (self.sem, self.cnt, self)


class Sched:
    def __init__(self, nc, stack):
        self.nc = nc
        self.stack = stack
        self.nsems = 0
        self.pe = EngQ(self, nc.tensor, "pe", is_pe=True)
        self.act = EngQ(self, nc.scalar, "act")
        self.dve = EngQ(self, nc.vector, "dve")
        self.pool = EngQ(self, nc.gpsimd, "pool")
        self.sp = EngQ(self, nc.sync, "sp", dma_only=True)
        self.queues = [self.pe, self.act, self.dve, self.pool, self.sp]
        self.dma_sems = []
        self.free_dsems = []
        self.free_dsems_sw = []
        self.phase_dsems = []

    def new_sem(self, name):
        self.nsems += 1
        return self.stack.enter_context(self.nc.semaphore(f"{name}_{self.nsems}"))

    def dsem(self, name, sw=False):
        pool = self.free_dsems_sw if sw else self.free_dsems
        if pool:
            s = pool.pop()
        else:
            s = DmaSem(self, name)
            s.sw = sw
            self.dma_sems.append(s)
        self.phase_dsems.append(s)
        return s

    def release_phase(self):
        for d in self.phase_dsems:
            (self.free_dsems_sw if d.sw else self.free_dsems).append(d)
        self.phase_dsems = []

    def _deps(self, q, reads, writes):
        for b in reads:
            q.wait(b.w, raw=True)
            if b.excl:
                for t in b.r.values():
                    q.wait(t, rar=True)
        for b in writes:
            q.wait(b.w, raw=True)
            for t in b.r.values():
                q.wait(t, raw=False)
            for t in b.rd:
                q.wait(t, raw=False)

    def op(self, q, reads, writes, fn):
        self._deps(q, reads, writes)
        ins = fn()
        tok = q.mark(ins)
        for b in reads:
            b.r[q.name] = tok
        for b in writes:
            b.w = tok
            b.r = {}
            b.rd = []
        return tok

    def dma(self, q, reads, writes, dsem, out, in_, **kw):
        assert dsem.sw == (q is self.pool), "DMA semaphore kind does not match the issuing queue"
        self._deps(q, reads, writes)
        ins = q.eng.dma_start(out=out, in_=in_, **kw)
        dsem.cnt += 16
        ins.then_inc(dsem.h, 16)
        tok = (dsem.h, dsem.cnt, None)
        for b in reads:
            b.rd.append(tok)
        for b in writes:
            b.w = tok
            b.r = {}
            b.rd = []
        return tok

    def barrier(self):
        toks = []
        for q in self.queues:
            if q.sem is not None and q.cnt > 0:
                toks.append((q.sem, q.cnt, None))
        for s in self.dma_sems:
            if s.cnt > 0:
                toks.append((s.h, s.cnt, None))
        for q in self.queues:
            for t in toks:
                q.wait(t)


class RR:
    def __init__(self, name, n):
        self.bufs = [Buf(f"{name}{i}") for i in range(n)]
        self.n = n
        self.i = 0
        self.base = 0

    def next(self):
        k = self.i % self.n
        self.i += 1
        return k + self.base, self.bufs[k]


WNAMES = ["ffn1_pre_g", "ffn1_post_g", "ffn1_w_gate", "ffn1_w_up", "ffn1_w_down",
          "mix_pre_g", "mix_post_g", "w_in", "sink", "q_norm_g", "kv_norm_g", "w_uq", "w_ukv",
          "w_a_out", "w_b_out", "w_o",
          "ffn2_pre_g", "ffn2_post_g", "ffn2_w_gate", "ffn2_w_up", "ffn2_w_down"]
WSHAPES = {
    "ffn1_pre_g": [DEPTH, D], "ffn1_post_g": [DEPTH, D], "ffn1_w_gate": [DEPTH, D, DFF],
    "ffn1_w_up": [DEPTH, D, DFF], "ffn1_w_down": [DEPTH, DFF, D],
    "mix_pre_g": [DEPTH, D], "mix_post_g": [DEPTH, D], "w_in": [DEPTH, D, IN_COLS], "sink": [DEPTH, 8],
    "q_norm_g": [DEPTH, 384], "kv_norm_g": [DEPTH, 256], "w_uq": [DEPTH, 384, 768],
    "w_ukv": [DEPTH, 256, 1024], "w_a_out": [DEPTH, 512, D], "w_b_out": [DEPTH, 512, D],
    "w_o": [DEPTH, D, D],
    "ffn2_pre_g": [DEPTH, D], "ffn2_post_g": [DEPTH, D], "ffn2_w_gate": [DEPTH, D, DFF],
    "ffn2_w_up": [DEPTH, D, DFF], "ffn2_w_down": [DEPTH, DFF, D],
}


def build_program(n_phases=8, dbg_stop=None):
    nc = bass.Bass("TRN2", target_bir_lowering=False)
    x_in = nc.dram_tensor("x", [NT, D], F32, kind="ExternalInput").ap()
    y = nc.dram_tensor("y", [NT, D], F32, kind="ExternalOutput").ap()
    class LazyW(dict):
        def __missing__(self, n):
            self[n] = nc.dram_tensor(n, WSHAPES[n], F32, kind="ExternalInput").ap()
            return self[n]
    W = LazyW()
    WSHAPES["rel_bias"] = [32, 8]
    nc.used_weights = W
    CSH = {"c_ident": [128, 128], "c_onehot2": [33, 640], "c_rope": [NT, 32], "c_flags": [128, 2]}

    class LazyC(dict):
        def __missing__(self, n):
            self[n] = nc.dram_tensor(n, CSH[n], F32, kind="ExternalInput").ap()
            return self[n]
    C = LazyC()
    nc.used_consts = C

    with ExitStack() as stack:
        S = Sched(nc, stack)
        block = stack.enter_context(nc.Block())
        ps = stack.enter_context(nc.psum_tensor("ps", [128, 8, 512], F32))
        pb = [Buf(f"pb{i}", excl=True) for i in range(8)]
        xsrc0 = [Buf(f"xi{s}") for s in range(NSUB)]
        yb = [Buf(f"y{s}") for s in range(NSUB)]

        _uid = [0]

        def SB(stk, name, shape, dt):
            _uid[0] += 1
            return stk.enter_context(nc.sbuf_tensor(f"{name}_{_uid[0]}", shape, dt))

        ident = stack.enter_context(nc.sbuf_tensor("ident", [128, 128], BF16))
        identb = Buf("ident")
        S.dma(S.pool, [], [identb], S.dsem("ident", sw=True), ident[:], C["c_ident"][:, :])

        def rstd_ops(ss_ap, rs_ap, ssb, rsb, n):
            S.op(S.act, [ssb, epsb], [rsb],
                 lambda: nc.scalar.activation(out=rs_ap, in_=ss_ap, func=AF.Ln, scale=1.0 / n, bias=eps_t[:, 0:1]))
            S.op(S.act, [rsb], [rsb],
                 lambda: nc.scalar.activation(out=rs_ap, in_=rs_ap, func=AF.Exp, scale=-0.5))

        eps_t = stack.enter_context(nc.sbuf_tensor("eps_t", [128, 1], F32))
        epsb = Buf("eps")
        S.op(S.pool, [], [epsb], lambda: nc.gpsimd.memset(eps_t[:], EPS))

        def ffn_phase(l, pfx, src, srcb, dst, dstb):
            wg = W[f"{pfx}_w_gate"][l].rearrange("(k p) f -> p k f", p=128)
            wu = W[f"{pfx}_w_up"][l].rearrange("(k p) f -> p k f", p=128)
            wd = W[f"{pfx}_w_down"][l].rearrange("(f p) n -> p f n", p=128)
            with ExitStack() as st:
                T = 2048
                hT = SB(st, "hT", [128, 8, T], BF16)
                aT = SB(st, "aT", [128, NFC, T], BF16)
                wdt = SB(st, "wdt", [128, NFC, D], BF16)
                wgu = SB(st, "wgu", [128, 3, 2, 8, 128], BF16)
                xin = SB(st, "xin", [128, 3, D], F32)
                hn = SB(st, "hn", [128, 2, D], BF16)
                sg = SB(st, "sg", [128, 2, 512], BF16)
                junk = sg[:, :, :].rearrange("p a b -> p (a b)")
                ybuf = SB(st, "ybuf", [128, 2, D], F32)
                gcol = SB(st, "gcol", [128, 8], F32)
                gpost = SB(st, "gpost", [128, D], F32)
                stat = SB(st, "stat", [128, 16], F32)

                hTev = [Buf(f"hTe{i}") for i in range(4)]
                hTod = [Buf(f"hTo{i}") for i in range(4)]
                aTb = [Buf(f"aT{i}") for i in range(4)]
                wdb = [Buf("wd")]
                wgu_rr = RR("wgu", 3)
                wgu_sem = [S.dsem(f"wgu{i}", sw=True) for i in range(3)]
                xin_rr = RR("xin", 3)
                xin_rr3 = RR("xin3", 2)
                xin_rr3.bufs = xin_rr.bufs[0:2]
                xin_rr1 = RR("xin1", 1)
                xin_rr1.bufs = xin_rr.bufs[2:3]
                xin_rr1.base = 2
                xin_sem = [S.dsem(f"xin{i}") for i in range(3)]
                hn_rr = RR("hn", 2)
                sg_rr = RR("sg", 2)
                y_rr = RR("ybuf", 2)
                y_sem = [S.dsem(f"yst{i}") for i in range(2)]
                gsem = S.dsem("g")
                gsem2 = S.dsem("g2")
                wd_sem = S.dsem("wd", sw=True)
                gpreb, gpostb = Buf("gpre"), Buf("gpost")
                statb = [Buf(f"stat{i}") for i in range(16)]

                S.dma(S.sp, [], [gpreb], gsem, gcol[:, :], W[f"{pfx}_pre_g"][l].rearrange("(k p) -> p k", p=128),
                      allow_slow_non_contiguous=True)
                S.dma(S.sp, [], [gpostb], gsem2, gpost[:], W[f"{pfx}_post_g"][l:l + 1, :].to_broadcast([128, D]))
                S.op(S.pool, [gpostb], [gpostb],
                     lambda: nc.gpsimd.tensor_scalar(out=gpost[:], in0=gpost[:], scalar1=0.5, scalar2=None, op0=ALU.mult))
                wd_issued = [False]

                def issue_wd():
                    if wd_issued[0]:
                        return
                    wd_issued[0] = True
                    for fc in range(NFC):
                        S.dma(S.pool, [], [wdb[0]], wd_sem, wdt[:, fc, :], wd[:, fc, :])

                cvt["gen"] = None
                gu_plan = []
                for stn in range(NT // T):
                    for fc in range(NFC):
                        gu_plan.append((stn, fc))
                gu_state = {"next": 0, "slots": {}}

                def issue_gu(upto):
                    while gu_state["next"] <= min(upto, len(gu_plan) - 1):
                        i = gu_state["next"]
                        _, fc = gu_plan[i]
                        k, b = wgu_rr.next()
                        S.dma(S.pool, [], [b], wgu_sem[k], wgu[:, k, 0, :, :], wg[:, :, fc * 128:(fc + 1) * 128])
                        S.dma(S.pool, [], [b], wgu_sem[k], wgu[:, k, 1, :, :], wu[:, :, fc * 128:(fc + 1) * 128])
                        gu_state["slots"][i] = (k, b)
                        gu_state["next"] += 1

                if dbg_stop == "gains":
                    S.barrier()
                    return
                issue_gu(1)
                gcnt = 0
                ycnt = 0
                s1st = {}

                def s1_front(stn, s, rr):
                    sub = stn * (T // 128) + s
                    k, xb = rr.next()
                    S.dma(S.sp, [srcb[sub]], [xb], xin_sem[k], xin[:, k, :], src[sub * 128:(sub + 1) * 128, :])
                    sc = k
                    S.op(S.act, [xb], [statb[sc]] + sg_rr.bufs,
                         lambda: nc.scalar.activation(out=junk[:], in_=xin[:, k, :], func=AF.Square,
                                                      accum_out=stat[:, sc:sc + 1]))
                    rstd_ops(stat[:, sc:sc + 1], stat[:, sc + 3:sc + 4], statb[sc], statb[sc + 3], D)
                    hk, hb = hn_rr.next()
                    S.op(S.dve, [xb, statb[sc + 3]], [hb],
                         lambda: nc.vector.tensor_scalar(out=hn[:, hk, :], in0=xin[:, k, :],
                                                         scalar1=stat[:, sc + 3:sc + 4], scalar2=None,
                                                         op0=ALU.mult))
                    s1st[(stn, s)] = (hk, hb)

                def s1_back(stn, s):
                    hk, hb = s1st.pop((stn, s))
                    tb = 6 + (s % 2)
                    pst = ps[:, tb, :].bitcast(BF16)

                    def tr():
                        for kk in range(8):
                            ins = nc.tensor.transpose(out=pst[:, kk * 128:(kk + 1) * 128],
                                                      in_=hn[:, hk, kk * 128:(kk + 1) * 128], identity=ident[:])
                        return ins
                    S.op(S.pe, [hb, identb], [pb[tb]], tr)

                    def cpa():
                        for kk in range(0, 8, 1):
                            ins = nc.scalar.activation(out=hT[:, kk, s * 128:(s + 1) * 128],
                                                       in_=pst[:, kk * 128:(kk + 1) * 128], func=AF.Copy,
                                                       scale=gcol[:, kk:kk + 1])
                        return ins

                    def cpv():
                        for kk in range(0, 8, 1):
                            ins = nc.vector.tensor_scalar(out=hT[:, kk, s * 128:(s + 1) * 128],
                                                          in0=pst[:, kk * 128:(kk + 1) * 128],
                                                          scalar1=gcol[:, kk:kk + 1], scalar2=None, op0=ALU.mult)
                        return ins
                    hTe = hTev[s // 4]
                    hTo = hTod[s // 4]
                    if tb == 6:
                        S.op(S.act, [pb[tb], gpreb], [hTe, hTo], cpa)
                    else:
                        S.op(S.dve, [pb[tb], gpreb], [hTe, hTo], cpv)

                NST = NT // T
                for stn in range(NST):
                    if stn == 0:
                        for s in range(T // 128):
                            s1_front(stn, s, xin_rr)
                            s1_back(stn, s)
                    for fc in range(NFC):
                        gi = stn * NFC + fc
                        issue_gu(gi + 2)
                        if gi == 1:
                            issue_wd()
                        if l == 0 and n_phases >= 3 and gi >= 3:
                            if cvt["gen"] is None:
                                cvt["gen"] = convert_mixer_weights(0 if pfx == "ffn1" else 1)
                            next(cvt["gen"], None)
                        wk, wb = gu_state["slots"][gi]
                        for tt in range(T // 512):
                            gbk = 2 * (gcnt % 2)
                            gcnt += 1

                            def mm(which, bank):
                                for kk in range(8):
                                    ins = nc.tensor.matmul(ps[:, bank, :], lhsT=wgu[:, wk, which, kk, :],
                                                           rhs=hT[:, kk, tt * 512:(tt + 1) * 512],
                                                           start=(kk == 0), stop=(kk == 7))
                                return ins
                            S.op(S.pe, [wb, hTev[tt], hTod[tt]], [pb[gbk]], lambda: mm(0, gbk))
                            S.op(S.pe, [wb, hTev[tt], hTod[tt]], [pb[gbk + 1]], lambda: mm(1, gbk + 1))
                            sk, sb = sg_rr.next()
                            S.op(S.act, [pb[gbk]], [sb],
                                 lambda: nc.scalar.activation(out=sg[:, sk, :], in_=ps[:, gbk, :], func=AF.Silu))
                            S.op(S.dve, [sb, pb[gbk + 1]], [aTb[tt]],
                                 lambda: nc.vector.tensor_tensor(out=aT[:, fc, tt * 512:(tt + 1) * 512],
                                                                 in0=sg[:, sk, :], in1=ps[:, gbk + 1, :], op=ALU.mult))
                    issue_wd()
                    if stn == NST - 1 and cvt["gen"] is not None:
                        for _ in cvt["gen"]:
                            pass
                    if dbg_stop == "s2":
                        S.barrier()
                        return
                    nsub = T // 128
                    pend = {}

                    inter = (stn + 1 < NST)
                    rr3 = xin_rr3 if inter else xin_rr
                    la3 = 1 if inter else 2

                    def ld(s):
                        if s >= nsub:
                            return
                        sub = stn * nsub + s
                        k, xb = rr3.next()
                        S.dma(S.sp, [srcb[sub]], [xb], xin_sem[k], xin[:, k, :], src[sub * 128:(sub + 1) * 128, :])
                        pend[s] = (k, xb)
                    for s_ in range(la3):
                        ld(s_)
                    if inter:
                        s1_front(stn + 1, 0, xin_rr1)
                    for s in range(nsub):
                        sub = stn * nsub + s
                        ld(s + la3)
                        xk, xb = pend.pop(s)
                        yk, ybb = y_rr.next()
                        for half in range(2):
                            bank = 4 + (ycnt % 2)
                            ycnt += 1

                            def dn():
                                for fc in range(NFC):
                                    ins = nc.tensor.matmul(ps[:, bank, :], lhsT=aT[:, fc, s * 128:(s + 1) * 128],
                                                           rhs=wdt[:, fc, half * 512:(half + 1) * 512],
                                                           start=(fc == 0), stop=(fc == NFC - 1))
                                return ins
                            S.op(S.pe, [aTb[s // 4]] + wdb, [pb[bank]], dn)
                            S.op(S.act, [pb[bank]], [ybb],
                                 lambda: nc.scalar.copy(out=ybuf[:, yk, half * 512:(half + 1) * 512], in_=ps[:, bank, :]))
                            c = 8 + yk * 2 + half
                            S.op(S.act, [pb[bank]], [statb[c], sg_rr.bufs[0]],
                                 lambda: nc.scalar.activation(out=junk[:, 0:512], in_=ps[:, bank, :], func=AF.Square,
                                                              accum_out=stat[:, c:c + 1]))
                        if inter:
                            s1_back(stn + 1, s)
                            if s + 1 < nsub:
                                s1_front(stn + 1, s + 1, xin_rr1)
                        c0 = 8 + yk * 2
                        c2 = 12 + yk
                        S.op(S.dve, [statb[c0], statb[c0 + 1]], [statb[c2]],
                             lambda: nc.vector.tensor_tensor(out=stat[:, c2:c2 + 1], in0=stat[:, c0:c0 + 1],
                                                             in1=stat[:, c0 + 1:c0 + 2], op=ALU.add))
                        rstd_ops(stat[:, c2:c2 + 1], stat[:, c2 + 2:c2 + 3], statb[c2], statb[c2 + 2], D)
                        S.op(S.dve, [ybb, statb[c2 + 2], gpostb], [ybb],
                             lambda: nc.vector.scalar_tensor_tensor(out=ybuf[:, yk, :], in0=ybuf[:, yk, :],
                                                                    scalar=stat[:, c2 + 2:c2 + 3], in1=gpost[:],
                                                                    op0=ALU.mult, op1=ALU.mult))
                        S.op(S.dve, [ybb, xb], [ybb],
                             lambda: nc.vector.tensor_tensor(out=ybuf[:, yk, :], in0=ybuf[:, yk, :],
                                                             in1=xin[:, xk, :], op=ALU.add))
                        S.dma(S.sp, [ybb], [dstb[sub]], y_sem[yk], dst[sub * 128:(sub + 1) * 128, :], ybuf[:, yk, :])
                S.barrier()

        kbT_d = nc.dram_tensor("kbT_d", [8, 96, NT], BF16, kind="Internal").ap()
        vb_d = nc.dram_tensor("vb_d", [128, 8, NSUB, 65], BF16, kind="Internal").ap()
        kaT_d = nc.dram_tensor("kaT_d", [64, 2, NT], BF16, kind="Internal").ap()
        va_d = nc.dram_tensor("va_d", [128, 2, NSUB, 65], BF16, kind="Internal").ap()
        ev_d = nc.dram_tensor("ev_d", [8, 640], BF16, kind="Internal").ap()
        wab_d = nc.dram_tensor("wab_d", [DEPTH, 8, 64, 2, 8, 128], BF16, kind="Internal").ap()
        wg_d = nc.dram_tensor("wg_d", [DEPTH, 8, 128, 2, 8, 128], BF16, kind="Internal").ap()
        wcv_b = [Buf("wcv0"), Buf("wcv1")]

        def convert_mixer_weights(l):
            sem = S.dsem(f"wcv{l}", sw=True)
            win_l = W["w_in"][l].rearrange("(k p) c -> p k c", p=128)
            wav = W["w_a_out"][l].rearrange("(h d) n -> d h n", d=64)
            wbv = W["w_b_out"][l].rearrange("(h d) n -> d h n", d=64)
            for dc in range(8):
                cs_ = slice(dc * 128, (dc + 1) * 128)
                S.dma(S.pool, [], [wcv_b[l]], sem, wab_d[l, dc, :, 0, :, :], wav[:, :, cs_])
                yield
                S.dma(S.pool, [], [wcv_b[l]], sem, wab_d[l, dc, :, 1, :, :], wbv[:, :, cs_])
                yield
                S.dma(S.pool, [], [wcv_b[l]], sem, wg_d[l, dc, :, 0, :, :], win_l[:, :, 1440 + dc * 128:1440 + (dc + 1) * 128])
                yield
                S.dma(S.pool, [], [wcv_b[l]], sem, wg_d[l, dc, :, 1, :, :], win_l[:, :, 2464 + dc * 128:2464 + (dc + 1) * 128])
                yield
        cvt = {"gen": None}

        kvd_b = [Buf(f"kvd{t}") for t in range(NT // 512)]
        biasd_b = Buf("biasd")
        ebd_b = Buf("ebd")

        def prologue():
            with ExitStack() as st:
                rb = SB(st, "rb", [33, 8], F32)
                oh = SB(st, "oh", [33, 640], F32)
                evs = SB(st, "evs", [8, 640], BF16)
                rbb, ohb, evb = Buf("rb"), Buf("oh"), Buf("evs")
                S.op(S.pool, [], [rbb], lambda: nc.gpsimd.memset(rb[:], -30000.0))
                S.dma(S.sp, [], [rbb], S.dsem("rb"), rb[0:32, :], W["rel_bias"][:, :])
                S.dma(S.sp, [], [ohb], S.dsem("oh"), oh[:, :], C["c_onehot2"][:, :])
                S.op(S.pe, [rbb, ohb], [pb[0]],
                     lambda: nc.tensor.matmul(ps[0:8, 0, :], lhsT=rb[:, :], rhs=oh[:, 0:512], start=True, stop=True))
                S.op(S.pe, [rbb, ohb], [pb[1]],
                     lambda: nc.tensor.matmul(ps[0:8, 1, 0:128], lhsT=rb[:, :], rhs=oh[:, 512:640], start=True, stop=True))
                S.op(S.act, [pb[0]], [evb],
                     lambda: nc.scalar.activation(out=evs[:, 0:512], in_=ps[0:8, 0, :], func=AF.Exp))
                S.op(S.act, [pb[1]], [evb],
                     lambda: nc.scalar.activation(out=evs[:, 512:640], in_=ps[0:8, 1, 0:128], func=AF.Exp))
                S.dma(S.sp, [evb], [ebd_b], S.dsem("evst"), ev_d[:, :], evs[:, :])
                S.barrier()

        def nt_front(R, sub, src, srcb):
            k, xb = R["xin_rr"].next()
            xin = R["xin"]
            stat = R["stat"]
            statb = R["statb"]
            S.dma(S.sp, [srcb[sub]], [xb], R["xin_sem"][k], xin[:, k, :], src[sub * 128:(sub + 1) * 128, :])
            sc = k
            S.op(S.act, [xb], [statb[sc]] + R["junkb"],
                 lambda: nc.scalar.activation(out=R["junk"], in_=xin[:, k, :], func=AF.Square,
                                              accum_out=stat[:, sc:sc + 1]))
            rstd_ops(stat[:, sc:sc + 1], stat[:, sc + 3:sc + 4], statb[sc], statb[sc + 3], D)
            hk, hb = R["hn_rr"].next()
            hn = R["hn"]
            S.op(S.dve, [xb, statb[sc + 3]], [hb],
                 lambda: nc.vector.tensor_scalar(out=hn[:, hk, :], in0=xin[:, k, :],
                                                 scalar1=stat[:, sc + 3:sc + 4], scalar2=None, op0=ALU.mult))
            return (hk, hb, sub)

        def nt_back(R, st_, hT, hTb, col0, eng=None):
            hk, hb, sub = st_
            hn = R["hn"]
            tb = 6 + (sub % 2)
            pst = ps[:, tb, :].bitcast(BF16)

            def tr():
                for kk in range(8):
                    ins = nc.tensor.transpose(out=pst[:, kk * 128:(kk + 1) * 128],
                                              in_=hn[:, hk, kk * 128:(kk + 1) * 128], identity=ident[:])
                return ins
            S.op(S.pe, [hb, identb], [pb[tb]], tr)

            def cpa():
                for kk in range(8):
                    ins = nc.scalar.activation(out=hT[:, kk, col0:col0 + 128],
                                               in_=pst[:, kk * 128:(kk + 1) * 128], func=AF.Copy,
                                               scale=R["gcol"][:, kk:kk + 1])
                return ins

            def cpv():
                for kk in range(8):
                    ins = nc.vector.tensor_scalar(out=hT[:, kk, col0:col0 + 128], in0=pst[:, kk * 128:(kk + 1) * 128],
                                                  scalar1=R["gcol"][:, kk:kk + 1], scalar2=None, op0=ALU.mult)
                return ins
            if tb == 6 and eng != "dve":
                S.op(S.act, [pb[tb], R["gcolb"]], [hTb], cpa)
            else:
                S.op(S.dve, [pb[tb], R["gcolb"]], [hTb], cpv)

        def norm_transpose(R, sub, src, srcb, hT, hTb, col0, keep_x=False):
            nt_back(R, nt_front(R, sub, src, srcb), hT, hTb, col0)

        def kv_phase(l):
            win = W["w_in"][l].rearrange("(k p) c -> p k c", p=128)
            with ExitStack() as st:
                hT2 = SB(st, "hT", [128, 2, 8, 512], BF16)
                wkv = SB(st, "wkv", [128, 8, 544], BF16)
                wukv = SB(st, "wukv", [128, 2, 1024], BF16)
                xin = SB(st, "xin", [128, 3, D], F32)
                hn = SB(st, "hn", [128, 2, D], BF16)
                junk = SB(st, "junk", [128, D], BF16)
                stat = SB(st, "stat", [128, 16], F32)
                gcol = SB(st, "gcol", [128, 8], F32)
                kvg = SB(st, "kvg", [128, 256], F32)
                cs = SB(st, "cs", [128, NSUB, 32], F32)
                ckvn = SB(st, "ckvn", [128, 4, 256], BF16)
                ckvnT = SB(st, "ckvnT", [128, 2, 512], BF16)
                ckvf = SB(st, "ckvf", [128, 4, 256], F32)
                ckvf_rr = RR("ckvf", 4)
                krt = SB(st, "krt", [128, 4, 96], BF16)
                rtmp = SB(st, "rtmp", [128, 4, 16], F32)
                kaT_sb = SB(st, "kaT_sb", [64, 2, 2, 512], BF16)
                va_sb = SB(st, "va_sb", [128, 2, 2, 4, 65], BF16)
                kbT_sb = SB(st, "kbT_sb", [96, 2, 8, 512], BF16)
                vb_sb = SB(st, "vb_sb", [128, 2, 8, 4, 65], BF16)

                R = dict(xin=xin, xin_rr=RR("xin", 3), xin_sem=[S.dsem(f"xin{i}") for i in range(3)], stat=stat,
                         statb=[Buf(f"stat{i}") for i in range(16)], hn=hn, hn_rr=RR("hn", 2), gcol=gcol,
                         gcolb=Buf("gcol"), junk=junk[:, :], junkb=[Buf("junk")])
                statb = R["statb"]
                S.dma(S.sp, [], [R["gcolb"]], S.dsem("gcol"), gcol[:, :],
                      W["mix_pre_g"][l].rearrange("(k p) -> p k", p=128), allow_slow_non_contiguous=True)
                kvgb, csb, wkvb, wukvb = Buf("kvg"), Buf("cs"), Buf("wkv"), Buf("wukv")
                S.dma(S.sp, [], [kvgb], S.dsem("kvg"), kvg[:, :], W["kv_norm_g"][l:l + 1, :].to_broadcast([128, 256]))
                S.dma(S.sp, [], [csb], S.dsem("cs"), cs[:, :, :], C["c_rope"].rearrange("(s p) e -> p s e", p=128))
                wsem = S.dsem("wkv", sw=True)
                for (c0, c1, o0) in ((512, 640, 0), (640, 768, 128), (1152, 1408, 256), (1408, 1440, 512)):
                    S.dma(S.pool, [], [wkvb], wsem, wkv[:, :, o0:o0 + (c1 - c0)], win[:, :, c0:c1])
                wu = W["w_ukv"][l].rearrange("(k p) (h t d) -> p k t h d", p=128, t=2, d=64)
                wsem2 = S.dsem("wukv", sw=True)
                for t in range(2):
                    for k in range(2):
                        S.dma(S.pool, [], [wukvb], wsem2,
                              wukv[:, k, t * 512:(t + 1) * 512].rearrange("p (h d) -> p h d", d=64), wu[:, k, t, :, :])
                onesb = Buf("ones")

                def ms():
                    nc.gpsimd.memset(krt[:], 0.0)
                    nc.gpsimd.memset(va_sb[:], 1.0)
                    return nc.gpsimd.memset(vb_sb[:], 1.0)
                S.op(S.pool, [], [onesb], ms)
                hTbs = [Buf("hT0"), Buf("hT1")]
                ckvnb = RR("ckvn", 4)
                ckvnTb = Buf("ckvnT")
                krtb = RR("krt", 4)
                rtb = Buf("rtmp")
                out_rr = RR("kvout", 2)
                st_sems = [[S.dsem(f"kvst{i}_{j}") for j in range(4)] for i in range(2)]
                bcnt = [0]

                def bank4():
                    b = bcnt[0] % 4
                    bcnt[0] += 1
                    return b

                for j in range(4):
                    norm_transpose(R, j, y, yb, hT2[:, 0, :, :], hTbs[0], j * 128)
                frs = {}
                for tq in range(NT // 512):
                    ok, ob = out_rr.next()
                    hT = hT2[:, tq % 2, :, :]
                    hTb = hTbs[tq % 2]
                    for kv in range(2):
                        bank = bank4()

                        def mm_a():
                            for kk in range(8):
                                ins = nc.tensor.matmul(ps[0:64, bank, :], lhsT=wkv[:, kk, kv * 64:(kv + 1) * 64],
                                                       rhs=hT[:, kk, :], start=(kk == 0), stop=(kk == 7))
                            return ins
                        S.op(S.pe, [wkvb, hTb], [pb[bank]], mm_a)
                        S.op(S.act, [pb[bank]], [ob],
                             lambda: nc.scalar.copy(out=kaT_sb[:, ok, kv, :], in_=ps[0:64, bank, :]))
                    if dbg_stop == "only_kv_a":
                        S.barrier()
                        return
                    bst = {}

                    def bc_mm(j):
                        bank = bank4()

                        def mm_b():
                            for kk in range(8):
                                ins = nc.tensor.matmul(ps[:, bank, 0:416], lhsT=hT[:, kk, j * 128:(j + 1) * 128],
                                                       rhs=wkv[:, kk, 128:544], start=(kk == 0), stop=(kk == 7))
                            return ins
                        S.op(S.pe, [wkvb, hTb], [pb[bank]], mm_b)
                        bst[j] = bank

                    def bc_chain(j):
                        sub = tq * 4 + j
                        bank = bst[j]
                        S.op(S.dve, [pb[bank], onesb], [ob],
                             lambda: nc.vector.tensor_copy(out=va_sb[:, ok, :, j, 0:64],
                                                           in_=ps[:, bank, 0:128].rearrange("p (k d) -> p k d", d=64)))
                        kk_, kb_ = krtb.next()
                        x1 = ps[:, bank, 384:400]
                        x2 = ps[:, bank, 400:416]
                        cos = cs[:, sub, 0:16]
                        sin = cs[:, sub, 16:32]

                        def rope():
                            nc.vector.tensor_tensor(out=rtmp[:, 0, :], in0=x1, in1=cos, op=ALU.mult)
                            nc.vector.tensor_tensor(out=rtmp[:, 1, :], in0=x2, in1=sin, op=ALU.mult)
                            nc.vector.tensor_tensor(out=rtmp[:, 2, :], in0=x1, in1=sin, op=ALU.mult)
                            return nc.vector.tensor_tensor(out=rtmp[:, 3, :], in0=x2, in1=cos, op=ALU.mult)
                        S.op(S.dve, [pb[bank], csb], [rtb], rope)

                        def rope2():
                            nc.vector.tensor_tensor(out=krt[:, kk_, 64:80], in0=rtmp[:, 0, :], in1=rtmp[:, 1, :],
                                                    op=ALU.subtract)
                            return nc.vector.tensor_tensor(out=krt[:, kk_, 80:96], in0=rtmp[:, 2, :], in1=rtmp[:, 3, :],
                                                           op=ALU.add)
                        S.op(S.dve, [rtb, onesb], [kb_], rope2)
                        c = 6 + j
                        S.op(S.act, [pb[bank]], [statb[c]] + R["junkb"],
                             lambda: nc.scalar.activation(out=junk[:, 0:256], in_=ps[:, bank, 128:384], func=AF.Square,
                                                          accum_out=stat[:, c:c + 1]))
                        rstd_ops(stat[:, c:c + 1], stat[:, c + 4:c + 5], statb[c], statb[c + 4], 256)
                        ck, cb = ckvnb.next()
                        fk, fb = ckvf_rr.next()
                        S.op(S.act, [pb[bank], statb[c + 4]], [fb],
                             lambda: nc.scalar.activation(out=ckvf[:, fk, :], in_=ps[:, bank, 128:384], func=AF.Copy,
                                                          scale=stat[:, c + 4:c + 5]))
                        S.op(S.dve, [fb, kvgb], [cb],
                             lambda: nc.vector.tensor_tensor(out=ckvn[:, ck, :], in0=ckvf[:, fk, :], in1=kvg[:, :],
                                                             op=ALU.mult))
                        bst[("c", j)] = (ck, cb, kk_, kb_)

                    def bc_tr(j):
                        ck, cb, kk_, kb_ = bst[("c", j)]
                        tb = 6 + (j % 2)
                        pst = ps[:, tb, :].bitcast(BF16)

                        def tr2():
                            for kc in range(2):
                                nc.tensor.transpose(out=pst[:, kc * 128:(kc + 1) * 128],
                                                    in_=ckvn[:, ck, kc * 128:(kc + 1) * 128], identity=ident[:])
                            return nc.tensor.transpose(out=pst[0:96, 256:384], in_=krt[:, kk_, :], identity=ident[:])
                        S.op(S.pe, [cb, kb_, identb], [pb[tb]], tr2)
                        S.op(S.act, [pb[tb]], [ckvnTb],
                             lambda: nc.scalar.copy(out=ckvnT[:, :, j * 128:(j + 1) * 128],
                                                    in_=pst[:, 0:256].rearrange("p (k t) -> p k t", k=2)))
                        S.op(S.act, [pb[tb]], [ob],
                             lambda: nc.scalar.copy(
                                 out=kbT_sb[64:96, ok, :, j * 128:(j + 1) * 128],
                                 in_=pst[64:96, 256:384].unsqueeze(1).to_broadcast([32, 8, 128])))
                    nxt = tq + 1 < NT // 512
                    nhT = hT2[:, (tq + 1) % 2, :, :]
                    nhTb = hTbs[(tq + 1) % 2]
                    fr = frs.pop(tq + 1, {})
                    if nxt and 0 not in fr:
                        fr[0] = nt_front(R, (tq + 1) * 4 + 0, y, yb)
                        fr[1] = nt_front(R, (tq + 1) * 4 + 1, y, yb)
                    for j in range(4):
                        bc_mm(j)
                    for j in range(4):
                        bc_chain(j)
                    for j in range(4):
                        bc_tr(j)
                        if nxt:
                            nt_back(R, fr.pop(j), nhT, nhTb, j * 128, eng="dve")
                            if j + 2 < 4:
                                fr[j + 2] = nt_front(R, (tq + 1) * 4 + j + 2, y, yb)
                    if tq + 2 < NT // 512:
                        frs[tq + 2] = {0: nt_front(R, (tq + 2) * 4 + 0, y, yb), 1: nt_front(R, (tq + 2) * 4 + 1, y, yb)}
                    for h in range(8):
                        bank = bank4()

                        def mm_k():
                            for kc in range(2):
                                ins = nc.tensor.matmul(ps[0:64, bank, :], lhsT=wukv[:, kc, h * 64:(h + 1) * 64],
                                                       rhs=ckvnT[:, kc, :], start=(kc == 0), stop=(kc == 1))
                            return ins
                        S.op(S.pe, [wukvb, ckvnTb], [pb[bank]], mm_k)
                        if h % 2 == 0:
                            S.op(S.act, [pb[bank]], [ob],
                                 lambda: nc.scalar.copy(out=kbT_sb[0:64, ok, h, :], in_=ps[0:64, bank, :]))
                        else:
                            S.op(S.dve, [pb[bank]], [ob],
                                 lambda: nc.vector.tensor_copy(out=kbT_sb[0:64, ok, h, :], in_=ps[0:64, bank, :]))
                    for j in range(4):
                        bank = bank4()

                        def mm_v():
                            for kc in range(2):
                                ins = nc.tensor.matmul(ps[:, bank, :], lhsT=ckvnT[:, kc, j * 128:(j + 1) * 128],
                                                       rhs=wukv[:, kc, 512:1024], start=(kc == 0), stop=(kc == 1))
                            return ins
                        S.op(S.pe, [wukvb, ckvnTb], [pb[bank]], mm_v)
                        cpe = S.act if j % 2 == 0 else S.dve

                        def cpv():
                            o = vb_sb[:, ok, :, j, 0:64]
                            i_ = ps[:, bank, :].rearrange("p (h d) -> p h d", d=64)
                            if cpe is S.act:
                                return nc.scalar.copy(out=o, in_=i_)
                            return nc.vector.tensor_copy(out=o, in_=i_)
                        S.op(cpe, [pb[bank], onesb], [ob], cpv)
                    if dbg_stop == "only_kv_d":
                        S.barrier()
                        return
                    t0, t1 = tq * 512, (tq + 1) * 512
                    S.dma(S.sp, [ob], [kvd_b[tq]], st_sems[ok][0], kaT_d[:, :, t0:t1], kaT_sb[:, ok, :, :])
                    S.dma(S.sp, [ob], [kvd_b[tq]], st_sems[ok][1], va_d[:, :, tq * 4:(tq + 1) * 4, :], va_sb[:, ok, :, :, :])
                    S.dma(S.sp, [ob], [kvd_b[tq]], st_sems[ok][2], kbT_d[:, :, t0:t1].rearrange("h r t -> r h t"),
                          kbT_sb[:, ok, :, :])
                    S.dma(S.sp, [ob], [kvd_b[tq]], st_sems[ok][3], vb_d[:, :, tq * 4:(tq + 1) * 4, :], vb_sb[:, ok, :, :, :])
                S.barrier()

        def mix_phase(l):
            win = W["w_in"][l].rearrange("(k p) c -> p k c", p=128)
            NQT = NT // 512
            with ExitStack() as st:
                hT2 = SB(st, "hT", [128, 2, 8, 512], BF16)
                xin = SB(st, "xin", [128, 2, D], F32)
                hn = SB(st, "hn", [128, 2, D], BF16)
                stat = SB(st, "stat", [128, 32], F32)
                gcol = SB(st, "gcol", [128, 8], F32)
                qaT = SB(st, "qaT", [64, 8, 512], BF16)
                cqn = SB(st, "cqn", [128, 2, 384], BF16)
                cqnT = SB(st, "cqnT", [128, 3, 512], BF16)
                cqf = SB(st, "cqf", [128, 1, 384], F32)
                cqf_rr = RR("cqf", 1)
                qtm = SB(st, "qtm", [128, 2, 8, 96], BF16)
                rtmp = SB(st, "rtmp", [128, 4, 4, 16], F32)
                qbT = SB(st, "qbT", [96, 8, 512], BF16)
                kaw = SB(st, "kaw", [64, 2, 2, 768], BF16)
                vaw = SB(st, "vaw", [128, 2, 2, 6 * 65 + 64], BF16)
                EB = SB(st, "EB", [128, 3, 8, 128], BF16)
                Ef = SB(st, "Ef", [128, 2, 512], F32)
                junk = Ef[:, 0, :].bitcast(BF16)
                PTb = SB(st, "PT", [128, 6, 512], BF16)
                PTa = PTb
                osb = SB(st, "osb", [65, 3, 512], F32)
                oaT = SB(st, "oaT", [64, 8, 512], BF16)
                obT = SB(st, "obT", [64, 8, 512], BF16)
                kbh = SB(st, "kbh", [96, 2, NT], BF16)
                vbh = SB(st, "vbh", [128, 2, NSUB * 65 + 64], BF16)
                mT = SB(st, "mT", [128, 8, 512], BF16)
                sga = SB(st, "sga", [128, 512], F32)
                sgb = SB(st, "sgb", [128, 512], F32)
                ybuf = SB(st, "ybuf", [128, 2, D], F32)
                gpost = SB(st, "gpost", [128, D], F32)
                qg = SB(st, "qg", [128, 384], F32)
                cs = SB(st, "cs", [128, NSUB, 32], F32)
                esk = SB(st, "esk", [65, 8], F32)
                onesf = SB(st, "onesf", [65, 64], F32)
                flags = SB(st, "flags", [128, 2], F32)
                wqa = SB(st, "wqa", [128, 8, 512], BF16)
                wcq = SB(st, "wcq", [128, 8, 384], BF16)
                wuq = SB(st, "wuq", [128, 3, 768], BF16)
                wab = SB(st, "wab", [64, 2, 2, 8, 128], BF16)
                wg = SB(st, "wg", [128, 2, 2, 8, 128], BF16)
                wo = SB(st, "wo", [128, 8, D], BF16)

                R = dict(xin=xin, xin_rr=RR("xin", 2), xin_sem=[S.dsem(f"xin{i}") for i in range(2)], stat=stat,
                         statb=[Buf(f"stat{i}") for i in range(32)], hn=hn, hn_rr=RR("hn", 2), gcol=gcol,
                         gcolb=Buf("gcol"), junk=junk[:, :], junkb=[Buf("junk")])
                statb = R["statb"]
                S.dma(S.sp, [], [R["gcolb"]], S.dsem("gcol"), gcol[:, :],
                      W["mix_pre_g"][l].rearrange("(k p) -> p k", p=128), allow_slow_non_contiguous=True)
                gpostb, qgb, csb, eskb, onesb, flagb, EBb = (Buf(n) for n in ("gpost", "qg", "cs", "esk", "ones", "flag", "EB"))
                S.dma(S.sp, [], [gpostb], S.dsem("gpost"), gpost[:, :], W["mix_post_g"][l:l + 1, :].to_broadcast([128, D]))
                S.dma(S.sp, [], [qgb], S.dsem("qg"), qg[:, :], W["q_norm_g"][l:l + 1, :].to_broadcast([128, 384]))
                S.dma(S.sp, [], [csb], S.dsem("cs"), cs[:, :, :], C["c_rope"].rearrange("(s p) e -> p s e", p=128))
                S.dma(S.sp, [], [eskb], S.dsem("esk"), esk[64:65, :], W["sink"][l:l + 1, :])
                S.op(S.act, [eskb], [eskb], lambda: nc.scalar.activation(out=esk[64:65, :], in_=esk[64:65, :], func=AF.Exp))
                S.op(S.pool, [], [onesb], lambda: nc.gpsimd.memset(onesf[:], 1.0))
                padb = Buf("pad")

                def pad_ms():
                    nc.gpsimd.memset(vaw[:], 0.0)
                    return nc.gpsimd.memset(vbh[:], 0.0)
                S.op(S.pool, [], [padb], pad_ms)
                S.dma(S.sp, [], [flagb], S.dsem("flag"), flags[:, :], C["c_flags"][:, :])
                mTb = Buf("mT")
                ebsem = S.dsem("EB")
                ebtmp = mT[:, :, :].rearrange("p a b -> p (a b)")[:, 0:3072].rearrange("p (c h q) -> p c h q", c=3, h=8)
                for c in range(3):
                    for h in range(8):
                        src = bass.AP(tensor=ev_d.tensor, offset=h * 640 + 128 * c + 1, ap=[[1, 128], [1, 128]])
                        S.dma(S.sp, [ebd_b], [mTb], ebsem, ebtmp[:, c, h, :], src)
                for c in range(3):
                    e0 = ebtmp[:, c, :, :]
                    rv = bass.AP(tensor=e0.tensor, offset=e0.offset + 127, ap=[list(e0.ap[0]), [128, 8], [-1, 128]])
                    S.op(S.dve, [mTb], [EBb], lambda: nc.vector.tensor_copy(out=EB[:, c, :, :], in_=rv))
                wqab, wcqb, wuqb, wob = (Buf(n) for n in ("wqa", "wcq", "wuq", "wo"))
                S.dma(S.pool, [], [wqab], S.dsem("wqa", sw=True), wqa[:, :, :], win[:, :, 0:512])
                S.dma(S.pool, [], [wcqb], S.dsem("wcq", sw=True), wcq[:, :, :], win[:, :, 768:1152])
                S.dma(S.pool, [], [wuqb], S.dsem("wuq", sw=True), wuq[:, :, :], W["w_uq"][l].rearrange("(k p) c -> p k c", p=128))
                wov = W["w_o"][l].rearrange("(k p) c -> p k c", p=128)
                wosem = S.dsem("wo", sw=True)
                for kk in range(8):
                    S.dma(S.pool, [], [wob], wosem, wo[:, kk, :], wov[:, kk, :])

                dc_rr = RR("wdc", 2)
                dc_sem = [S.dsem("wdc0", sw=True), S.dsem("wdc1", sw=True)]
                dc_state = {"next": 0, "slots": {}}

                def issue_dc(upto):
                    while dc_state["next"] <= min(upto, NQT * 8 - 1):
                        g = dc_state["next"]
                        dc = g % 8
                        k, b = dc_rr.next()
                        S.dma(S.pool, [wcv_b[l]], [b], dc_sem[k], wab[:, k, :, :, :], wab_d[l, dc, :, :, :, :])
                        S.dma(S.pool, [wcv_b[l]], [b], dc_sem[k], wg[:, k, :, :, :], wg_d[l, dc, :, :, :, :])
                        dc_state["slots"][g] = (k, b)
                        dc_state["next"] += 1

                kv_rr = RR("kvh", 2)
                kv_sem = [S.dsem("kvh0"), S.dsem("kvh1")]
                kv_state = {"next": 0, "slots": {}}

                def issue_kvh(upto):
                    while kv_state["next"] <= min(upto, NQT * 8 - 1):
                        g = kv_state["next"]
                        h = g % 8
                        k, b = kv_rr.next()
                        S.dma(S.sp, kvd_b + [padb], [b], kv_sem[k], kbh[:, k, :], kbT_d[h, :, :])
                        S.dma(S.sp, kvd_b, [b], kv_sem[k], vbh[:, k, 0:NSUB * 65], vb_d[:, h, :, :].rearrange("p c e -> p (c e)"))
                        xlo = 16 if (g // 8) // 4 == 0 else 0
                        S.op(S.dve, [b, flagb], [b],
                             lambda: nc.vector.tensor_scalar(out=vbh[:, k, xlo * 65:(xlo + 16) * 65],
                                                             in0=vbh[:, k, xlo * 65:(xlo + 16) * 65],
                                                             scalar1=flags[:, 0:1], scalar2=None, op0=ALU.mult))
                        kv_state["slots"][g] = (k, b)
                        kv_state["next"] += 1

                hTbs = [Buf("hT0"), Buf("hT1")]
                qaTb, cqnTb, qbTb, oaTb, obTb = (Buf(n) for n in ("qaT", "cqnT", "qbT", "oaT", "obT"))
                cqn_rr = RR("cqn", 2)
                qtm_rr = RR("qtm", 2)
                rtb = Buf("rtmp")
                kw_rr = RR("kw", 2)
                kw_sem = [S.dsem("kw0"), S.dsem("kw1")]
                ef_rr = RR("Ef", 2)
                R["junkb"] = [ef_rr.bufs[0]]
                pta_rr = RR("PTa", 3)
                ptb_bufs = [Buf(f"PTb{i}") for i in range(6)]
                pta_rr.bufs = ptb_bufs[0:3]
                osb_rr = RR("osb", 3)
                sgab, sgbb, m1b = Buf("sga"), Buf("sgb"), Buf("m1")
                y_rr = RR("ybuf", 2)
                y_sem = [S.dsem("yst0"), S.dsem("yst1")]
                cnt = {"b4": 0, "ob": 0, "bc": 0, "yb": 0, "pair": 0}

                def bank4():
                    b = cnt["b4"] % 4
                    cnt["b4"] += 1
                    return b

                def norm_a(obank, sink_kv):
                    ok, ob_ = osb_rr.next()
                    if sink_kv is None:
                        S.op(S.dve, [pb[obank]], [ob_], lambda: nc.vector.tensor_copy(out=osb[:, ok, :], in_=ps[0:65, obank, :]))
                    else:
                        S.op(S.act, [pb[obank]], [ob_], lambda: nc.scalar.copy(out=osb[:, ok, :], in_=ps[0:65, obank, :]))

                    if sink_kv is not None:
                        S.op(S.dve, [ob_, eskb], [ob_], lambda: nc.vector.tensor_tensor(
                            out=osb[64:65, ok, :].rearrange("p (g q) -> p g q", g=4),
                            in0=osb[64:65, ok, :].rearrange("p (g q) -> p g q", g=4),
                            in1=esk[64:65, sink_kv * 4:(sink_kv + 1) * 4].unsqueeze(2).to_broadcast([1, 4, 128]),
                            op=ALU.add))
                        S.op(S.act, [ob_], [ob_],
                             lambda: nc.scalar.activation(out=osb[64:65, ok, :], in_=osb[64:65, ok, :], func=AF.Ln))
                        S.op(S.act, [ob_], [ob_],
                             lambda: nc.scalar.activation(out=osb[64:65, ok, :], in_=osb[64:65, ok, :], func=AF.Exp,
                                                          scale=-1.0))
                    else:
                        S.op(S.dve, [ob_], [ob_],
                             lambda: nc.vector.reciprocal(out=osb[64:65, ok, :], in_=osb[64:65, ok, :]))
                    return ok, ob_

                def norm_b(ok, ob_, dest, destb, bank=None):
                    if bank is None:
                        bcb = 6 + (cnt["bc"] % 2)
                        cnt["bc"] += 1
                    else:
                        bcb = bank
                    S.op(S.pe, [ob_, onesb], [pb[bcb]],
                         lambda: nc.tensor.matmul(ps[0:64, bcb, :], lhsT=onesf[64:65, 0:64], rhs=osb[64:65, ok, :],
                                                  start=True, stop=True))
                    S.op(S.dve, [ob_, pb[bcb]], [destb],
                         lambda: nc.vector.tensor_tensor(out=dest, in0=osb[0:64, ok, :], in1=ps[0:64, bcb, :], op=ALU.mult))

                issue_kvh(1)
                qst = {}

                def s123(qt):
                    hT = hT2[:, qt % 2, :, :]
                    hTb = hTbs[qt % 2]
                    f0 = nt_front(R, qt * 4 + 0, y, yb)
                    f1 = nt_front(R, qt * 4 + 1, y, yb)
                    yield
                    nt_back(R, f0, hT, hTb, 0)
                    f2 = nt_front(R, qt * 4 + 2, y, yb)
                    yield
                    nt_back(R, f1, hT, hTb, 128)
                    f3 = nt_front(R, qt * 4 + 3, y, yb)
                    yield
                    nt_back(R, f2, hT, hTb, 256)
                    yield
                    nt_back(R, f3, hT, hTb, 384)
                    lo, hi = max(0, 4 * qt - 1), min(NSUB, 4 * qt + 5)
                    b0 = 4 * qt - 1
                    wk, wkb = kw_rr.next()
                    rtiles = kvd_b[lo * 128 // 512:(hi * 128 - 1) // 512 + 1]
                    S.dma(S.sp, rtiles + [padb], [wkb], kw_sem[wk], kaw[:, wk, :, (lo - b0) * 128:(hi - b0) * 128],
                          kaT_d[:, :, lo * 128:hi * 128])
                    S.dma(S.sp, rtiles, [wkb], kw_sem[wk],
                          vaw[:, wk, :, 0:390].rearrange("p k (b e) -> p k b e", e=65)[:, :, lo - b0:hi - b0, :],
                          va_d[:, :, lo:hi, :])
                    qst[qt] = (b0, wk, wkb)
                    for h in range(8):
                        bank = bank4()

                        def mm_qa():
                            for kk in range(8):
                                ins = nc.tensor.matmul(ps[0:64, bank, :], lhsT=wqa[:, kk, h * 64:(h + 1) * 64],
                                                       rhs=hT[:, kk, :], start=(kk == 0), stop=(kk == 7))
                            return ins
                        S.op(S.pe, [wqab, hTb], [pb[bank]], mm_qa)
                        if h % 2 == 0:
                            S.op(S.act, [pb[bank]], [qaTb],
                                 lambda: nc.scalar.activation(out=qaT[:, h, :], in_=ps[0:64, bank, :], func=AF.Copy, scale=0.125))
                        else:
                            S.op(S.dve, [pb[bank]], [qaTb],
                                 lambda: nc.vector.tensor_scalar(out=qaT[:, h, :], in0=ps[0:64, bank, :], scalar1=0.125,
                                                                 scalar2=None, op0=ALU.mult))
                        if h % 4 == 3:
                            yield
                    cqst = {}

                    def cq_mm(j):
                        bank = bank4()

                        def mm_cq():
                            for kk in range(8):
                                ins = nc.tensor.matmul(ps[:, bank, 0:384], lhsT=hT[:, kk, j * 128:(j + 1) * 128],
                                                       rhs=wcq[:, kk, :], start=(kk == 0), stop=(kk == 7))
                            return ins
                        S.op(S.pe, [wcqb, hTb], [pb[bank]], mm_cq)
                        cqst[j] = bank

                    def cq_chain(j):
                        bank = cqst[j]
                        c = 8 + (j % 2)
                        S.op(S.act, [pb[bank]], [statb[c]] + R["junkb"],
                             lambda: nc.scalar.activation(out=junk[:, 0:384], in_=ps[:, bank, 0:384], func=AF.Square,
                                                          accum_out=stat[:, c:c + 1]))
                        rstd_ops(stat[:, c:c + 1], stat[:, c + 2:c + 3], statb[c], statb[c + 2], 384)
                        ck, cb = cqn_rr.next()
                        fk, fb = cqf_rr.next()
                        S.op(S.act, [pb[bank], statb[c + 2]], [fb],
                             lambda: nc.scalar.activation(out=cqf[:, fk, :], in_=ps[:, bank, 0:384], func=AF.Copy,
                                                          scale=stat[:, c + 2:c + 3]))
                        S.op(S.dve, [fb, qgb], [cb],
                             lambda: nc.vector.tensor_tensor(out=cqn[:, ck, :], in0=cqf[:, fk, :], in1=qg[:, :],
                                                             op=ALU.mult))
                        cqst[("c", j)] = (ck, cb)

                    def cq_tr(j):
                        ck, cb = cqst[("c", j)]
                        tb = 6 + (j % 2)
                        pst = ps[:, tb, :].bitcast(BF16)

                        def tr3():
                            for kc in range(3):
                                ins = nc.tensor.transpose(out=pst[:, kc * 128:(kc + 1) * 128],
                                                          in_=cqn[:, ck, kc * 128:(kc + 1) * 128], identity=ident[:])
                            return ins
                        S.op(S.pe, [cb, identb], [pb[tb]], tr3)
                        S.op(S.act, [pb[tb]], [cqnTb],
                             lambda: nc.scalar.copy(out=cqnT[:, :, j * 128:(j + 1) * 128],
                                                    in_=pst[:, 0:384].rearrange("p (k t) -> p k t", k=3)))
                    cq_mm(0)
                    cq_mm(1)
                    cq_chain(0)
                    cq_mm(2)
                    cq_chain(1)
                    cq_mm(3)
                    yield
                    cq_tr(0)
                    cq_chain(2)
                    cq_tr(1)
                    cq_chain(3)
                    yield
                    cq_tr(2)
                    cq_tr(3)

                    def q_mm(j):
                        sub = qt * 4 + j
                        qk, qb_ = qtm_rr.next()
                        cqst[("q", j)] = (qk, qb_)
                        for half in range(2):
                            bank = bank4()

                            def mm_q():
                                for kc in range(3):
                                    ins = nc.tensor.matmul(ps[:, bank, 0:384], lhsT=cqnT[:, kc, j * 128:(j + 1) * 128],
                                                           rhs=wuq[:, kc, half * 384:(half + 1) * 384],
                                                           start=(kc == 0), stop=(kc == 2))
                                return ins
                            S.op(S.pe, [wuqb, cqnTb], [pb[bank]], mm_q)
                            psv = ps[:, bank, 0:384].rearrange("p (h e) -> p h e", h=4)
                            x1 = psv[:, :, 64:80]
                            x2 = psv[:, :, 80:96]
                            cos = cs[:, sub, 0:16].unsqueeze(1).to_broadcast([128, 4, 16])
                            sin = cs[:, sub, 16:32].unsqueeze(1).to_broadcast([128, 4, 16])

                            def rope():
                                nc.vector.tensor_copy(out=qtm[:, qk, half * 4:(half + 1) * 4, 0:64], in_=psv[:, :, 0:64])
                                nc.vector.tensor_tensor(out=rtmp[:, 0, :, :], in0=x1, in1=cos, op=ALU.mult)
                                nc.vector.tensor_tensor(out=rtmp[:, 1, :, :], in0=x2, in1=sin, op=ALU.mult)
                                nc.vector.tensor_tensor(out=rtmp[:, 2, :, :], in0=x1, in1=sin, op=ALU.mult)
                                return nc.vector.tensor_tensor(out=rtmp[:, 3, :, :], in0=x2, in1=cos, op=ALU.mult)
                            S.op(S.dve, [pb[bank], csb], [rtb, qb_], rope)

                            def rope2():
                                nc.vector.tensor_tensor(out=qtm[:, qk, half * 4:(half + 1) * 4, 64:80], in0=rtmp[:, 0, :, :],
                                                        in1=rtmp[:, 1, :, :], op=ALU.subtract)
                                return nc.vector.tensor_tensor(out=qtm[:, qk, half * 4:(half + 1) * 4, 80:96],
                                                               in0=rtmp[:, 2, :, :], in1=rtmp[:, 3, :, :], op=ALU.add)
                            S.op(S.dve, [rtb], [qb_], rope2)

                    def q_tr(j):
                        qk, qb_ = cqst[("q", j)]
                        tb = 6 + (j % 2)
                        pst = ps[:, tb, :].bitcast(BF16)

                        def tr4():
                            for h in range(8):
                                ins = nc.tensor.transpose(out=pst[0:96, h * 128:(h + 1) * 128], in_=qtm[:, qk, h, :],
                                                          identity=ident[:])
                            return ins
                        S.op(S.pe, [qb_, identb], [pb[tb]], tr4)
                        S.op(S.act, [pb[tb]], [qbTb], lambda: nc.scalar.copy(
                            out=qbT[:, :, j * 128:(j + 1) * 128], in_=pst[0:96, :].rearrange("p (h t) -> p h t", h=8)))
                    q_mm(0)
                    q_mm(1)
                    yield
                    q_tr(0)
                    q_mm(2)
                    q_tr(1)
                    q_mm(3)
                    yield
                    q_tr(2)
                    q_tr(3)
                    yield

                def s45(qt, g7):
                    b0, wk, wkb = qst[qt]
                    wsteps = []
                    for j in range(4):
                        n = 4 * qt + j
                        for kv in range(2):
                            chunks = [c for c in range(3) if 0 <= n - 1 + c < NSUB]
                            for ci, c in enumerate(chunks):
                                wsteps.append((j, kv, ci, c, len(chunks)))
                    LAW = 2
                    winfo = {}
                    wobank = {}
                    pending = []

                    def flush(upto_i, keep=0):
                        while len(pending) > keep or (pending and pending[0][0] <= upto_i):
                            it_ = pending.pop(0)
                            norm_b(it_[1], it_[2], it_[3], it_[4], bank=(it_[5] if len(it_) > 5 else None))
                    for i in range(len(wsteps) + LAW):
                        if i < len(wsteps):
                            j, kv, ci, c, nch = wsteps[i]
                            n = 4 * qt + j
                            kb_ = n - 1 + c
                            bi = kb_ - b0
                            if ci == 0:
                                wobank[(j, kv)] = 4 + (cnt["ob"] % 2)
                                cnt["ob"] += 1
                            bank = cnt["b4"] % 2
                            cnt["b4"] += 1
                            next(g7, None)
                            S.op(S.pe, [wkb, qaTb], [pb[bank]],
                                 lambda: nc.tensor.matmul(ps[:, bank, :], lhsT=kaw[:, wk, kv, bi * 128:(bi + 1) * 128],
                                                          rhs=qaT[:, kv * 4:(kv + 1) * 4, j * 128:(j + 1) * 128],
                                                          start=True, stop=True))
                            ek, eb_ = ef_rr.next()
                            S.op(S.act, [pb[bank]], [eb_],
                                 lambda: nc.scalar.activation(out=Ef[:, ek, :], in_=ps[:, bank, :], func=AF.Exp))
                            pk, pbuf = pta_rr.next()
                            cross = (kb_ // 16) != (n // 16)
                            ebv = EB[:, c, kv * 4:(kv + 1) * 4, :]
                            efv = Ef[:, ek, :].rearrange("p (g q) -> p g q", g=4)
                            ptv = PTa[:, pk, :].rearrange("p (g q) -> p g q", g=4)
                            if cross:
                                S.op(S.dve, [eb_, EBb, flagb], [pbuf],
                                     lambda: nc.vector.scalar_tensor_tensor(out=ptv, in0=efv, scalar=flags[:, 0:1],
                                                                            in1=ebv, op0=ALU.mult, op1=ALU.mult))
                            else:
                                S.op(S.dve, [eb_, EBb], [pbuf],
                                     lambda: nc.vector.tensor_tensor(out=ptv, in0=efv, in1=ebv, op=ALU.mult))
                            winfo[i] = (pk, pbuf, bi)
                        ip = i - LAW
                        if ip >= 0:
                            j, kv, ci, c, nch = wsteps[ip]
                            pk, pbuf, bi = winfo.pop(ip)
                            obank = wobank[(j, kv)]
                            S.op(S.pe, [wkb, pbuf], [pb[obank]],
                                 lambda: nc.tensor.matmul(ps[:, obank, :], lhsT=vaw[:, wk, kv, bi * 65:bi * 65 + 128],
                                                          rhs=PTa[:, pk, :], start=(ci == 0), stop=(ci == nch - 1)))
                            if ci == nch - 1:
                                flush(i, keep=2)
                                ok, ob_ = norm_a(obank, kv)
                                pending.append((i + 7, ok, ob_, oaT[:, kv * 4:(kv + 1) * 4, j * 128:(j + 1) * 128], oaTb))
                        flush(i, keep=3)
                    flush(10 ** 9, keep=0)
                    for _ in g7:
                        pass
                    issue_dc(qt * 8)
                    steps = [(h, kp) for h in range(8) for kp in range(NSUB // 2)]
                    LA = 2
                    sc_ = (64 + 32) ** -0.5
                    info = {}
                    hstate = {}
                    for i in range(len(steps) + LA):
                        if i < len(steps):
                            h, kp = steps[i]
                            g = qt * 8 + h
                            if kp == LA + 1:
                                issue_kvh(g + 1)
                            if kp == 0:
                                obank = 6 + (h % 2)
                                hstate[h] = obank
                            hk, hb_ = kv_state["slots"][g]
                            bp = 2 * (cnt["pair"] % 3)
                            pp = 2 * (cnt["pair"] % 3)
                            cnt["pair"] += 1

                            def qk2():
                                for t in range(2):
                                    kc = 2 * kp + t
                                    ins = nc.tensor.matmul(ps[:, bp + t, :], lhsT=kbh[:, hk, kc * 128:(kc + 1) * 128],
                                                           rhs=qbT[:, h, :], start=True, stop=True)
                                return ins
                            S.op(S.pe, [hb_, qbTb], [pb[bp], pb[bp + 1]], qk2)
                            pbufs = [ptb_bufs[pp], ptb_bufs[pp + 1]]
                            S.op(S.act, [pb[bp], pb[bp + 1]], pbufs,
                                 lambda: nc.scalar.activation(out=PTb[:, pp:pp + 2, :], in_=ps[:, bp:bp + 2, :], func=AF.Exp,
                                                              scale=sc_))
                            info[i] = (pp, pbufs)
                        ip = i - LA
                        if ip >= 0:
                            h, kp = steps[ip]
                            g = qt * 8 + h
                            hk, hb_ = kv_state["slots"][g]
                            pp, pbufs = info.pop(ip)
                            obank = hstate[h]

                            def pv2():
                                for t in range(2):
                                    kc = 2 * kp + t
                                    ins = nc.tensor.matmul(ps[:, obank, :], lhsT=vbh[:, hk, kc * 65:kc * 65 + 128],
                                                           rhs=PTb[:, pp + t, :], start=(kc == 0), stop=(kc == NSUB - 1))
                                return ins
                            S.op(S.pe, [hb_] + pbufs, [pb[obank]], pv2)
                            if kp == NSUB // 2 - 1:
                                flush(i, keep=2)
                                ok, ob_ = norm_a(obank, None)
                                pending.append((i + 5, ok, ob_, obT[:, h, :], obTb, obank))
                        flush(i, keep=3)
                    flush(10 ** 9, keep=0)

                def s6(qt, gen):
                    hT = hT2[:, qt % 2, :, :]
                    hTb = hTbs[qt % 2]
                    for dc in range(8):
                        g = qt * 8 + dc
                        issue_dc(g + 1)
                        wk_, wdb_ = dc_state["slots"][g]
                        base = 4 * (dc % 2)

                        def mm_o(wt, which, src, bank):
                            for h in range(8):
                                ins = nc.tensor.matmul(ps[:, bank, :], lhsT=wt[:, wk_, which, h, :], rhs=src[:, h, :],
                                                       start=(h == 0), stop=(h == 7))
                            return ins
                        S.op(S.pe, [wdb_, hTb], [pb[base + 2]], lambda: mm_o(wg, 0, hT, base + 2))
                        S.op(S.pe, [wdb_, hTb], [pb[base + 3]], lambda: mm_o(wg, 1, hT, base + 3))
                        S.op(S.pe, [wdb_, obTb], [pb[base + 1]], lambda: mm_o(wab, 1, obT, base + 1))
                        S.op(S.pe, [wdb_, oaTb], [pb[base]], lambda: mm_o(wab, 0, oaT, base))
                        S.op(S.act, [pb[base + 2]], [sgab],
                             lambda: nc.scalar.activation(out=sga[:, :], in_=ps[:, base + 2, :], func=AF.Sigmoid))
                        S.op(S.act, [pb[base + 3]], [sgbb],
                             lambda: nc.scalar.activation(out=sgb[:, :], in_=ps[:, base + 3, :], func=AF.Sigmoid))
                        S.op(S.dve, [sgab, pb[base]], [sgab],
                             lambda: nc.vector.tensor_tensor(out=sga[:, :], in0=sga[:, :], in1=ps[:, base, :], op=ALU.mult))
                        S.op(S.dve, [sgbb, pb[base + 1]], [sgbb],
                             lambda: nc.vector.tensor_tensor(out=sgb[:, :], in0=sgb[:, :], in1=ps[:, base + 1, :], op=ALU.mult))
                        S.op(S.dve, [sgab, sgbb], [mTb],
                             lambda: nc.vector.tensor_tensor(out=mT[:, dc, :], in0=sga[:, :], in1=sgb[:, :], op=ALU.add))
                        next(gen, None)

                def s7(qt):
                    pend = {}

                    def ld(j):
                        if j >= 4:
                            return
                        sub = qt * 4 + j
                        k, xb = R["xin_rr"].next()
                        S.dma(S.sp, [yb[sub]], [xb], R["xin_sem"][k], xin[:, k, :], y[sub * 128:(sub + 1) * 128, :])
                        pend[j] = (k, xb)
                    ld(0)
                    for j in range(4):
                        sub = qt * 4 + j
                        ld(j + 1)
                        xk, xb = pend.pop(j)
                        yk, ybb = y_rr.next()
                        for half in range(2):
                            bank = 2 + cnt["yb"] % 2
                            cnt["yb"] += 1

                            def mm_wo():
                                for dc in range(8):
                                    ins = nc.tensor.matmul(ps[:, bank, :], lhsT=mT[:, dc, j * 128:(j + 1) * 128],
                                                           rhs=wo[:, dc, half * 512:(half + 1) * 512],
                                                           start=(dc == 0), stop=(dc == 7))
                                return ins
                            S.op(S.pe, [mTb, wob], [pb[bank]], mm_wo)
                            S.op(S.act, [pb[bank]], [ybb],
                                 lambda: nc.scalar.copy(out=ybuf[:, yk, half * 512:(half + 1) * 512], in_=ps[:, bank, :]))
                            c = 16 + yk * 2 + half
                            S.op(S.act, [pb[bank]], [statb[c]] + R["junkb"],
                                 lambda: nc.scalar.activation(out=junk[:, 0:512], in_=ps[:, bank, :], func=AF.Square,
                                                              accum_out=stat[:, c:c + 1]))
                            yield
                        c0 = 16 + yk * 2
                        c2 = 20 + yk
                        S.op(S.dve, [statb[c0], statb[c0 + 1]], [statb[c2]],
                             lambda: nc.vector.tensor_tensor(out=stat[:, c2:c2 + 1], in0=stat[:, c0:c0 + 1],
                                                             in1=stat[:, c0 + 1:c0 + 2], op=ALU.add))
                        rstd_ops(stat[:, c2:c2 + 1], stat[:, c2 + 2:c2 + 3], statb[c2], statb[c2 + 2], D)
                        S.op(S.dve, [ybb, statb[c2 + 2], gpostb], [ybb],
                             lambda: nc.vector.scalar_tensor_tensor(out=ybuf[:, yk, :], in0=ybuf[:, yk, :],
                                                                    scalar=stat[:, c2 + 2:c2 + 3], in1=gpost[:, :],
                                                                    op0=ALU.mult, op1=ALU.mult))
                        S.op(S.dve, [ybb, xb], [ybb],
                             lambda: nc.vector.tensor_tensor(out=ybuf[:, yk, :], in0=ybuf[:, yk, :],
                                                             in1=xin[:, xk, :], op=ALU.add))
                        S.dma(S.sp, [ybb], [yb[sub]], y_sem[yk], y[sub * 128:(sub + 1) * 128, :], ybuf[:, yk, :])
                        yield

                gen0 = s123(0)
                for _ in gen0:
                    pass
                g7 = iter(())
                for qt in range(NQT):
                    s45(qt, g7)
                    gen = s123(qt + 1) if qt + 1 < NQT else iter(())
                    s6(qt, gen)
                    for _ in gen:
                        pass
                    g7 = s7(qt)
                for _ in g7:
                    pass
                S.barrier()

        phases = [
            lambda: ffn_phase(0, "ffn1", x_in, xsrc0, y, yb),
            lambda: kv_phase(0),
            lambda: mix_phase(0),
            lambda: ffn_phase(0, "ffn2", y, yb, y, yb),
            lambda: ffn_phase(1, "ffn1", y, yb, y, yb),
            lambda: kv_phase(1),
            lambda: mix_phase(1),
            lambda: ffn_phase(1, "ffn2", y, yb, y, yb),
        ]
        if dbg_stop == "only_pro":
            prologue()
            phases = []
        if dbg_stop is not None and dbg_stop.startswith("only_kv"):
            kv_phase(0)
            phases = []
        if n_phases >= 3 and dbg_stop is None:
            prologue()
        S.barrier()
        S.release_phase()
        for i, ph in enumerate(phases):
            if i >= n_phases:
                break
            if ph is not None:
                ph()
                S.barrier()
                S.release_phase()
        S.barrier()
    return nc


def shard_tokens(x_prompt, x_sample):
    xs = []
    for c in range(N_CORES):
        if c < 4:
            xs.append(np.ascontiguousarray(x_sample[c]))
        else:
            j = 2 * (c - 4)
            xs.append(np.ascontiguousarray(np.concatenate([x_prompt[j], x_prompt[j + 1]], axis=0)))
    return xs


_HC = {}


def _t5_bucket_np(rel):
    import jax
    import jax.numpy as jnp
    with jax.default_device(jax.devices("cpu")[0]):
        rel = jnp.asarray(rel, dtype=jnp.int32)
        half = 16
        max_exact = 8
        ret = jnp.where(rel > 0, half, 0)
        n = jnp.abs(rel)
        nf = jnp.maximum(n, 1).astype(jnp.float32)
        large = max_exact + (jnp.log(nf / max_exact) / math.log(128 / max_exact) * (half - max_exact)).astype(jnp.int32)
        large = jnp.minimum(large, half - 1)
        return np.asarray(ret + jnp.where(n < max_exact, n, large))


def host_constants(core):
    if "ident" not in _HC:
        _HC["ident"] = np.eye(128, dtype=np.float32)
        rel = np.arange(640) - 256
        bucket = _t5_bucket_np(rel)
        inside = np.abs(rel) <= 128
        oh = np.zeros((33, 640), np.float32)
        for b in range(32):
            oh[b] = ((bucket == b) & inside).astype(np.float32)
        oh[32] = (~inside).astype(np.float32)
        _HC["onehot2"] = np.ascontiguousarray(oh)
        inv_freq = (10000.0 ** (-np.arange(0, 32, 2, dtype=np.float32) / np.float32(32))).astype(np.float32)
        for kind, seqlen in (("s", 4096), ("p", 2048)):
            pos = (np.arange(NT) % seqlen).astype(np.float32)
            ang = (pos[:, None] * inv_freq[None, :]).astype(np.float32)
            _HC["rope_" + kind] = np.ascontiguousarray(
                np.concatenate([np.cos(ang), np.sin(ang)], axis=1).astype(np.float32))
        fl = np.zeros((128, 2), np.float32)
        fl[:, 0] = 1.0
        _HC["flags_s"] = fl
        fl = np.zeros((128, 2), np.float32)
        fl[:, 1] = -30000.0
        _HC["flags_p"] = fl
    kind = "s" if core < 4 else "p"
    return {"c_ident": _HC["ident"], "c_onehot2": _HC["onehot2"], "c_rope": _HC["rope_" + kind],
            "c_flags": _HC["flags_" + kind]}


def kernel(**inputs):
    x_prompt = np.asarray(inputs["x_prompt"], dtype=np.float32)
    x_sample = np.asarray(inputs["x_sample"], dtype=np.float32)
    xs = shard_tokens(x_prompt, x_sample)
    nc = build_program()
    wmap = {n: np.ascontiguousarray(np.asarray(inputs[n], dtype=np.float32)) for n in nc.used_weights}
    in_maps = []
    for c in range(N_CORES):
        m = dict(wmap)
        m["x"] = xs[c]
        hc = host_constants(c)
        for n in nc.used_consts:
            m[n] = hc[n]
        in_maps.append(m)
    res = run_bass_kernel_spmd(nc, in_maps, core_ids=list(range(N_CORES)))
    ys = [np.asarray(r["y"]) for r in res.results]
    y_sample = np.stack(ys[0:4], axis=0)
    y_prompt = np.stack([ys[4 + j // 2][(j % 2) * 2048:(j % 2 + 1) * 2048] for j in range(8)], axis=0)
    return (y_prompt.astype(np.float32), y_sample.astype(np.float32))
```
